# Optimizing a Trainium2 kernel written in Bass

```python
import math
import jax, jax.numpy as jnp
from jax import lax
import numpy as np

D_MODEL = 1024
BATCH = 8
SEQ = 4096
DEPTH = 2

M_HEADS = 4
M_V_WIDTH = 2 * D_MODEL
M_QK_WIDTH = D_MODEL
M_DV = M_V_WIDTH // M_HEADS
M_DK = M_QK_WIDTH // M_HEADS
M_CHUNK = 128
CONV_K = 4
A_HEADS = D_MODEL // 128
A_DH = 64
A_QK_WIDTH = A_HEADS * 2 * A_DH
A_V_WIDTH = A_HEADS * 2 * A_DH
Q_BLOCK = 128
REL_BUCKETS = 32
REL_MAX_DIST = 128
EPS = 1e-6

SPLITS = (M_QK_WIDTH, M_QK_WIDTH, M_V_WIDTH, M_V_WIDTH, M_V_WIDTH, M_HEADS, M_HEADS,
          A_QK_WIDTH, A_QK_WIDTH, A_V_WIDTH, A_V_WIDTH, D_MODEL, D_MODEL)
N_IN = 3 * M_QK_WIDTH - M_QK_WIDTH + 3 * M_V_WIDTH + 2 * M_HEADS + 2 * A_QK_WIDTH + 2 * A_V_WIDTH + 2 * D_MODEL

kernel_name = "hybrid_mlstm_diffattn_gated_block"


def rmsnorm(x, g):
    xf = x.astype(jnp.float32)
    y = xf * lax.rsqrt(jnp.mean(xf * xf, axis=-1, keepdims=True) + EPS)
    return (y * g.astype(jnp.float32)).astype(x.dtype)


def head_rmsnorm(x, n_heads, g):
    B, S, W = x.shape
    xf = x.astype(jnp.float32).reshape(B, S, n_heads, W // n_heads)
    xf = xf * lax.rsqrt(jnp.mean(xf * xf, axis=-1, keepdims=True) + EPS)
    return (xf.reshape(B, S, W) * g.astype(jnp.float32)).astype(x.dtype)


def causal_dwconv(x, w, b):
    S = x.shape[1]
    xp = jnp.pad(x, ((0, 0), (CONV_K - 1, 0), (0, 0)))
    y = xp[:, CONV_K - 1:CONV_K - 1 + S] * w[CONV_K - 1] + b
    for j in range(CONV_K - 1):
        y = y + xp[:, j:j + S] * w[j]
    return y


def t5_bucket(n):
    max_exact = REL_BUCKETS // 2
    nf = jnp.maximum(n, 1).astype(jnp.float32)
    large = max_exact + (jnp.log(nf / max_exact) / math.log(REL_MAX_DIST / max_exact)
                         * (REL_BUCKETS - max_exact)).astype(jnp.int32)
    large = jnp.minimum(large, REL_BUCKETS - 1)
    return jnp.where(n < max_exact, n, large)


def mlstm_chunkwise(q, k, v, i_pre, f_pre):
    in_dtype = v.dtype
    B, S, H, dk = q.shape
    dv = v.shape[-1]
    nc = S // M_CHUNK
    L = M_CHUNK
    f32 = jnp.float32

    def chunks4(t):
        return t.astype(f32).reshape(B, nc, L, H, t.shape[-1]).transpose(1, 0, 3, 2, 4)

    def chunks3(t):
        return t.astype(f32).reshape(B, nc, L, H).transpose(1, 0, 3, 2)

    qc = chunks4(q)
    kc = chunks4(k) * (dk ** -0.5)
    vc = chunks4(v)
    ic = chunks3(i_pre)
    lfc = jax.nn.log_sigmoid(chunks3(f_pre))
    causal = jnp.tril(jnp.ones((L, L), dtype=bool))

    def step(carry, xs):
        C, n, m = carry
        qx, kx, vx, ix, lfx = xs
        b = jnp.cumsum(lfx, axis=-1)
        D = b[..., :, None] - b[..., None, :] + ix[..., None, :]
        D = jnp.where(causal, D, -jnp.inf)
        inter = b + m[..., None]
        m_t = jnp.maximum(inter, jnp.max(D, axis=-1))
        w_inter = jnp.exp(inter - m_t)
        s = jnp.einsum('bhtd,bhsd->bhts', qx, kx) * jnp.exp(D - m_t[..., None])
        num = (w_inter[..., None] * jnp.einsum('bhvd,bhtd->bhtv', C, qx)
               + jnp.einsum('bhts,bhsv->bhtv', s, vx))
        den = w_inter * jnp.einsum('bhd,bhtd->bht', n, qx) + jnp.sum(s, axis=-1)
        h = num / jnp.maximum(jnp.abs(den), jnp.exp(-m_t))[..., None]
        bL = b[..., -1]
        g = bL[..., None] - b + ix
        m_new = jnp.maximum(bL + m, jnp.max(g, axis=-1))
        decay = jnp.exp(bL + m - m_new)
        wk = jnp.exp(g - m_new[..., None])
        C_new = decay[..., None, None] * C + jnp.einsum('bhsv,bhsd->bhvd', vx * wk[..., None], kx)
        n_new = decay[..., None] * n + jnp.einsum('bhs,bhsd->bhd', wk, kx)
        return (C_new, n_new, m_new), h

    init = (jnp.zeros((B, H, dv, dk), f32), jnp.zeros((B, H, dk), f32), jnp.zeros((B, H), f32))
    _, hs = lax.scan(step, init, (qc, kc, vc, ic, lfc))
    return hs.transpose(1, 0, 3, 2, 4).reshape(B, S, H, dv).astype(in_dtype)


def diff_attention(q, k, v, lam, rel_bias):
    in_dtype = v.dtype
    f32 = jnp.float32
    S = q.shape[1]
    qh = q.astype(f32).transpose(0, 2, 3, 1, 4) * (A_DH ** -0.5)
    kh = k.astype(f32).transpose(0, 2, 3, 1, 4)
    vh = v.astype(f32).transpose(0, 2, 1, 3)
    table = rel_bias.astype(f32)
    outs = []
    for blk in range(S // Q_BLOCK):
        q0 = blk * Q_BLOCK
        end = q0 + Q_BLOCK
        rel = jnp.arange(q0, end)[:, None] - jnp.arange(end)[None, :]
        bias = table[t5_bucket(jnp.maximum(rel, 0))]
        bias = jnp.where((rel >= 0)[..., None], bias, -jnp.inf).transpose(2, 0, 1)
        logits = jnp.einsum('bhmqd,bhmkd->bhmqk', qh[:, :, :, q0:end], kh[:, :, :, :end])
        p = jax.nn.softmax(logits + bias[None, :, None], axis=-1)
        a = p[:, :, 0] - lam * p[:, :, 1]
        outs.append(jnp.einsum('bhqk,bhkd->bhqd', a, vh[:, :, :end]))
    o = jnp.concatenate(outs, axis=2)
    return o.transpose(0, 2, 1, 3).astype(in_dtype)


def setup_inputs(seed: int = 0) -> dict:
    key = jax.random.key(seed)
    ks = jax.random.split(key, 24)
    nrm = jax.random.normal
    f32 = jnp.float32
    x = nrm(ks[0], (BATCH, SEQ, D_MODEL), f32)
    c = nrm(ks[1], (BATCH, D_MODEL), f32)
    norm_g = 1.0 + 0.05 * nrm(ks[2], (DEPTH, D_MODEL), f32)
    w_ada = 0.5 * D_MODEL ** -0.5 * nrm(ks[3], (DEPTH, D_MODEL, 3 * D_MODEL), f32)
    b_ada = 0.02 * nrm(ks[4], (DEPTH, 3 * D_MODEL), f32)
    w_in = D_MODEL ** -0.5 * nrm(ks[5], (DEPTH, D_MODEL, N_IN), f32)
    b_i = 0.1 * nrm(ks[6], (DEPTH, M_HEADS), f32)
    b_f = jnp.linspace(3.0, 6.0, M_HEADS, dtype=f32)[None] + 0.1 * nrm(ks[7], (DEPTH, M_HEADS), f32)
    b_if = jnp.concatenate([b_i, b_f], axis=-1)
    conv_w = CONV_K ** -0.5 * nrm(ks[8], (DEPTH, CONV_K, 2 * M_QK_WIDTH), f32)
    conv_b = 0.02 * nrm(ks[9], (DEPTH, 2 * M_QK_WIDTH), f32)
    mnorm_g = 1.0 + 0.05 * nrm(ks[10], (DEPTH, M_V_WIDTH), f32)
    w_proj_m = M_V_WIDTH ** -0.5 * nrm(ks[11], (DEPTH, M_V_WIDTH, D_MODEL), f32)
    lam_q1 = 0.1 * nrm(ks[12], (DEPTH, A_DH), f32)
    lam_k1 = 0.1 * nrm(ks[13], (DEPTH, A_DH), f32)
    lam_q2 = 0.1 * nrm(ks[14], (DEPTH, A_DH), f32)
    lam_k2 = 0.1 * nrm(ks[15], (DEPTH, A_DH), f32)
    anorm_g = 1.0 + 0.05 * nrm(ks[16], (DEPTH, A_V_WIDTH), f32)
    w_proj_a = A_V_WIDTH ** -0.5 * nrm(ks[17], (DEPTH, A_V_WIDTH, D_MODEL), f32)
    w_out = D_MODEL ** -0.5 * nrm(ks[18], (DEPTH, D_MODEL, D_MODEL), f32)
    rel_bias = 0.5 * nrm(ks[19], (REL_BUCKETS, A_HEADS), f32)
    final_g = 1.0 + 0.05 * nrm(ks[20], (D_MODEL,), f32)
    return {"x": x, "c": c, "norm_g": norm_g, "w_ada": w_ada, "b_ada": b_ada, "w_in": w_in,
            "b_if": b_if, "conv_w": conv_w, "conv_b": conv_b, "mnorm_g": mnorm_g, "w_proj_m": w_proj_m,
            "lam_q1": lam_q1, "lam_k1": lam_k1, "lam_q2": lam_q2, "lam_k2": lam_k2, "anorm_g": anorm_g,
            "w_proj_a": w_proj_a, "w_out": w_out, "rel_bias": rel_bias, "final_g": final_g}


def reference(x, c, norm_g, w_ada, b_ada, w_in, b_if, conv_w, conv_b, mnorm_g, w_proj_m,
              lam_q1, lam_k1, lam_q2, lam_k2, anorm_g, w_proj_a, w_out, rel_bias, final_g):
    B, S, _ = x.shape
    split_idx = np.cumsum(np.array(SPLITS))[:-1].tolist()
    c_act = jax.nn.silu(c)
    for l in range(DEPTH):
        mod = c_act @ w_ada[l] + b_ada[l]
        shift, scale, gate = jnp.split(mod, 3, axis=-1)
        h = rmsnorm(x, norm_g[l]) * (1.0 + scale[:, None]) + shift[:, None]

        proj = h @ w_in[l]
        (qm, km, vm, om, zm, im, fm, qa, ka, va, za, gm, ga) = jnp.split(proj, split_idx, axis=-1)

        qk = jax.nn.silu(causal_dwconv(jnp.concatenate([qm, km], axis=-1), conv_w[l], conv_b[l]))
        qm, km = jnp.split(qk, 2, axis=-1)
        im = im + b_if[l, :M_HEADS]
        fm = fm + b_if[l, M_HEADS:]
        hm = mlstm_chunkwise(qm.reshape(B, S, M_HEADS, M_DK), km.reshape(B, S, M_HEADS, M_DK),
                             vm.reshape(B, S, M_HEADS, M_DV), im, fm)
        hm = jax.nn.sigmoid(om) * hm.reshape(B, S, M_V_WIDTH)
        hm = head_rmsnorm(hm, M_HEADS, mnorm_g[l]) * jax.nn.silu(zm)
        ym = hm @ w_proj_m[l]

        lam_init = 0.8 - 0.6 * math.exp(-0.3 * l)
        lam = (jnp.exp(jnp.sum(lam_q1[l].astype(jnp.float32) * lam_k1[l].astype(jnp.float32)))
               - jnp.exp(jnp.sum(lam_q2[l].astype(jnp.float32) * lam_k2[l].astype(jnp.float32)))
               + lam_init)
        ha = diff_attention(qa.reshape(B, S, A_HEADS, 2, A_DH), ka.reshape(B, S, A_HEADS, 2, A_DH),
                            va.reshape(B, S, A_HEADS, 2 * A_DH), lam, rel_bias)
        ha = head_rmsnorm(ha.reshape(B, S, A_V_WIDTH), A_HEADS, anorm_g[l]) * (1.0 - lam_init)
        ha = ha * jax.nn.silu(za)
        ya = ha @ w_proj_a[l]

        y = jax.nn.sigmoid(gm) * ym + jax.nn.sigmoid(ga) * ya
        x = x + gate[:, None] * (y @ w_out[l])
    return rmsnorm(x, final_g)
```

```python
import numpy as np
import ml_dtypes
from contextlib import ExitStack
import concourse.bass as bass
import concourse.mybir as mybir
from concourse.bass_utils import run_bass_kernel_spmd

F32 = mybir.dt.float32
BF16 = mybir.dt.bfloat16
AF = mybir.ActivationFunctionType
ALU = mybir.AluOpType
AX = mybir.AxisListType

D = 1024
S = 4096
NT = S // 128
DEPTH = 2
N_IN = 14344
EPS = 1e-6
ENGS = ["tensor", "vector", "scalar", "gpsimd", "sync"]
NDS = 24


class _Op:
    __slots__ = ("fn", "waits", "inc", "incval", "dsem", "dinc")

    def __init__(self, fn):
        self.fn = fn
        self.waits = []
        self.inc = False
        self.incval = 0
        self.dsem = None
        self.dinc = 0


class Prog:
    def __init__(self, nc, stack):
        self.nc = nc
        self.sems = {e: stack.enter_context(nc.semaphore("s_" + e)) for e in ENGS[:4]}
        self.dsems = [stack.enter_context(nc.semaphore("d%d" % i)) for i in range(NDS)]
        self.dval = [0] * NDS
        self.dnext = 0
        self.inc_count = {e: 0 for e in ENGS[:4]}
        self.nblock = 0
        self._reset()

    def _reset(self):
        self.ops = {e: [] for e in ENGS}
        self.res = {}
        self.cw = {e: {} for e in ENGS}
        self.dw = {e: {} for e in ENGS}

    def _need(self, eng, op, tok):
        if tok[0] == "c":
            _, src, idx = tok
            if self.cw[eng].get(src, -1) >= idx:
                return
            self.cw[eng][src] = idx
            self.ops[src][idx].inc = True
            op.waits.append(tok)
        else:
            _, s, v = tok
            if self.dw[eng].get(s, 0) >= v:
                return
            self.dw[eng][s] = v
            op.waits.append(tok)

    def _rec(self, eng, op, tok, rkey, reads, writes):
        is_dma = tok[0] == "d"
        deps = []
        for r in reads:
            st = self.res.get(r)
            if st is not None and st[0] is not None:
                w = st[0]
                if not (w[0] == "c" and w[1] == eng and eng == "tensor" and not is_dma):
                    deps.append(w)
        for wk in writes:
            st = self.res.get(wk)
            if st is not None:
                w = st[0]
                if w is not None and (is_dma or w[0] == "d" or w[1] != eng):
                    deps.append(w)
                for k, t in st[1].items():
                    if is_dma or t[0] == "d" or t[1] != eng:
                        deps.append(t)
        for t in deps:
            self._need(eng, op, t)
        for r in reads:
            st = self.res.setdefault(r, [None, {}])
            st[1][rkey] = tok
        for wk in writes:
            self.res[wk] = [tok, {}]

    def op(self, eng, fn, reads=(), writes=()):
        o = _Op(fn)
        idx = len(self.ops[eng])
        tok = ("c", eng, idx)
        self._rec(eng, o, tok, eng, reads, writes)
        self.ops[eng].append(o)
        return tok

    def dma(self, eng, fn, reads=(), writes=()):
        o = _Op(fn)
        s = self.dnext
        self.dnext = (self.dnext + 1) % NDS
        if self.dval[s] > 0:
            self._need(eng, o, ("d", s, self.dval[s]))
        self.dval[s] += 16
        tok = ("d", s, self.dval[s])
        o.dsem = s
        self._rec(eng, o, tok, ("d", s), reads, writes)
        self.ops[eng].append(o)
        return tok

    def flush(self):
        nc = self.nc
        fin = _Op(None)
        for s in range(NDS):
            if self.dval[s] > 0:
                self._need("sync", fin, ("d", s, self.dval[s]))
        self.ops["sync"].append(fin)
        for e in ENGS[:4]:
            c = self.inc_count[e]
            for o in self.ops[e]:
                if o.inc:
                    c += 1
                    o.incval = c
            self.inc_count[e] = c
        ops = self.ops
        sems, dsems = self.sems, self.dsems

        def emit(e, name):
            for o in ops[name]:
                for t in o.waits:
                    if t[0] == "c":
                        e.wait_ge(sems[t[1]], ops[t[1]][t[2]].incval)
                    else:
                        e.wait_ge(dsems[t[1]], t[2])
                if o.fn is None:
                    continue
                ins = o.fn(e)
                if o.dsem is not None:
                    ins.then_inc(dsems[o.dsem], 16)
                elif o.inc:
                    ins.then_inc(sems[name], 1)

        self.nblock += 1
        with nc.Block() as block:
            @block.tensor
            def _(e):
                emit(e, "tensor")

            @block.vector
            def _(e):
                emit(e, "vector")

            @block.scalar
            def _(e):
                emit(e, "scalar")

            @block.gpsimd
            def _(e):
                emit(e, "gpsimd")

            @block.sync
            def _(e):
                emit(e, "sync")
        self._reset()


C_QM, C_KM, C_VM, C_OM, C_ZM = 0, 1024, 2048, 4096, 6144
C_IM, C_FM = 8192, 8196
C_QA, C_KA, C_VA, C_ZA, C_GM, C_GA = 8200, 9224, 10248, 11272, 12296, 13320


class K:
    pass


def build(cfg):
    nc = bass.Bass("TRN2", target_bir_lowering=False)
    k = K()
    k.nc = nc
    k.cfg = cfg
    depth = cfg.get("depth", DEPTH)
    stop = cfg.get("stop", None)
    with ExitStack() as top:
        P = Prog(nc, top)
        k.P = P

        def din(name, shape, dt=F32):
            return nc.dram_tensor(name, list(shape), dt, kind="ExternalInput").ap()

        def dscr(name, shape, dt):
            kind = "ExternalOutput" if name in cfg.get("dump", ()) else "Internal"
            return nc.dram_tensor(name, list(shape), dt, kind=kind).ap()

        uid = [0]

        def sb(stack, name, shape, dt=F32):
            uid[0] += 1
            return stack.enter_context(nc.sbuf_tensor("%s_u%d" % (name, uid[0]), list(shape), dt))

        def ps(stack, name, shape, dt=F32):
            uid[0] += 1
            return stack.enter_context(nc.psum_tensor("%s_u%d" % (name, uid[0]), list(shape), dt))

        x_in = din("x", [S, D])
        crep = din("crep", [128, 8, 128])
        w_ada = din("w_ada", [DEPTH, D, 3 * D])
        bada_fm = din("bada_fm", [DEPTH, 128, 16])
        bgate_b = din("bgate_b", [DEPTH, 128, D])
        ng_fm = din("ng_fm", [DEPTH, 128, 8])
        w_in = din("w_in", [DEPTH, D, N_IN])
        b_i = din("b_i", [DEPTH, 4, 1])
        b_f = din("b_f", [DEPTH, 4, 1])
        convw_fm = din("convw_fm", [DEPTH, 128, 16, 4])
        convb_fm = din("convb_fm", [DEPTH, 128, 16])
        mg_b = din("mg_b", [DEPTH, 128, 2048])
        mg_fm = din("mg_fm", [DEPTH, 128, 16])
        w_pm = din("w_pm", [DEPTH, 2048, D])
        w_pa = din("w_pa", [DEPTH, D, D])
        w_o = din("w_o", [DEPTH, D, D])
        lam_b = din("lam_b", [DEPTH, 128, 4, 64])
        ag_fm = din("ag_fm", [DEPTH, 128, 8])
        tbias = din("tbias", [8, 128, 640])
        fg_b = din("fg_b", [128, D])
        ident_bf_d = din("ident_bf", [128, 128], BF16)
        ident_f_d = din("ident_f", [128, 128])
        maskT_d = din("maskT", [128, 128])
        sel4_d = din("sel4", [4, 4, 128])
        out = nc.dram_tensor("out", [S, D], F32, kind="ExternalOutput").ap()

        x1 = dscr("x1", [S, D], F32)
        qkm_T = dscr("qkm_T", [2048, S], BF16)
        vm_s = dscr("vm_s", [S, 2048], BF16)
        gom_s = dscr("gom_s", [S, 2048], BF16)
        zm_s = dscr("zm_s", [S, 2048], BF16)
        qa_T = dscr("qa_T", [1024, S], BF16)
        ka_T = dscr("ka_T", [1024, S], BF16)
        va_s = dscr("va_s", [S, 1024], BF16)
        za_T = dscr("za_T", [1024, S], BF16)
        sgm_T = dscr("sgm_T", [1024, S], BF16)
        sga_T = dscr("sga_T", [1024, S], BF16)
        ha_T = dscr("ha_T", [1024, S], BF16)
        hm_T = dscr("hm_T", [2048, S], BF16)
        hT_dump = dscr("hT_dump", [1024, S], BF16) if "hT_dump" in cfg.get("dump", ()) else None

        ident_bf = sb(top, "ident_bf_s", [128, 128], BF16)
        ident_f = sb(top, "ident_f_s", [128, 128])
        modfm = sb(top, "modfm", [128, DEPTH, 16])
        gsfm = sb(top, "gsfm", [128, DEPTH, 8])
        gateb = sb(top, "gateb", [128, DEPTH, D])

        def finish(name):
            P.flush()
            return stop == name

        with ExitStack() as ph:
            ca = sb(ph, "ca", [128, 8, 128])
            wada = sb(ph, "wada", [128, 8, 3 * D])
            tmp16 = sb(ph, "tmp16", [128, 16])
            bfm = sb(ph, "bfm", [128, 16])
            ngt = sb(ph, "ngt", [128, 8])
            bgt = sb(ph, "bgt", [128, D])
            pmod = ps(ph, "pmod", [128, 512])
            pg0 = ps(ph, "pg0", [128, 512])
            pg1 = ps(ph, "pg1", [128, 512])
            P.dma("sync", lambda e: e.dma_start(out=ident_bf[:], in_=ident_bf_d), writes=["ident_bf"])
            P.dma("sync", lambda e: e.dma_start(out=ident_f[:], in_=ident_f_d), writes=["ident_f"])
            P.dma("sync", lambda e: e.dma_start(out=ca[:], in_=crep), writes=["ca"])
            P.op("scalar", lambda e: e.activation(out=ca[:], in_=ca[:], func=AF.Silu), reads=["ca"], writes=["ca"])
            scut = cfg.get("scut", 99)
            for l in range(depth if scut > 0 else 0):
                for kc in range(8):
                    P.dma("sync", lambda e, l=l, kc=kc: e.dma_start(
                        out=wada[:, kc, :], in_=w_ada[l, kc * 128:(kc + 1) * 128, :]),
                        writes=[("wada", kc)])
                P.dma("sync", lambda e, l=l: e.dma_start(out=bfm[:], in_=bada_fm[l]), writes=["bfm"])
                P.dma("sync", lambda e, l=l: e.dma_start(out=ngt[:], in_=ng_fm[l]), writes=["ngt"])
                P.dma("sync", lambda e, l=l: e.dma_start(out=bgt[:], in_=bgate_b[l]), writes=["bgt"])
                if scut < 2:
                    continue
                for f in range(16):
                    for kc in range(8):
                        P.op("tensor", lambda e, f=f, kc=kc: e.matmul(
                            pmod[:, 2 * f:2 * f + 2], lhsT=wada[:, kc, f * 128:(f + 1) * 128],
                            rhs=ca[:, kc, 0:2], start=(kc == 0), stop=(kc == 7)),
                            reads=[("wada", kc), "ca"], writes=["pmod"])
                if scut < 3:
                    continue
                for half, pg in enumerate((pg0, pg1)):
                    for kc in range(8):
                        P.op("tensor", lambda e, kc=kc, half=half, pg=pg: e.matmul(
                            pg[:, :], lhsT=ca[:, kc, :],
                            rhs=wada[:, kc, 2048 + half * 512:2048 + (half + 1) * 512],
                            start=(kc == 0), stop=(kc == 7)),
                            reads=[("wada", kc), "ca"], writes=[("pg", half)])
                if scut < 4:
                    continue
                pm_v = pmod[:, 0:32].rearrange("p (f two) -> p f two", two=2)[:, :, 0]
                P.op("vector", lambda e, l=l, pm_v=pm_v: e.tensor_tensor(
                    out=modfm[:, l, :], in0=pm_v, in1=bfm[:], op=ALU.add),
                    reads=["pmod", "bfm"], writes=[("modfm", l)])
                P.op("vector", lambda e, l=l: e.scalar_tensor_tensor(
                    out=gsfm[:, l, :], in0=modfm[:, l, 8:16], scalar=1.0, in1=ngt[:],
                    op0=ALU.add, op1=ALU.mult),
                    reads=[("modfm", l), "ngt"], writes=[("gsfm", l)])
                for half, pg in enumerate((pg0, pg1)):
                    P.op("vector", lambda e, l=l, half=half, pg=pg: e.tensor_tensor(
                        out=gateb[:, l, half * 512:(half + 1) * 512], in0=pg[:, :],
                        in1=bgt[:, half * 512:(half + 1) * 512], op=ALU.add),
                        reads=[("pg", half), "bgt"], writes=[("gateb", l, half)])
            if finish("S"):
                return nc

        for l in range(depth):
            last = (l == depth - 1)
            x_src = x_in if l == 0 else x1
            x_dst = out if last else x1
            with ExitStack() as lay:
                cols = (sb(lay, "wkcol%d" % l, [128, NT, 4]), sb(lay, "wkcolb%d" % l, [128, NT, 4], BF16),
                        sb(lay, "necol%d" % l, [128, NT, 4]), sb(lay, "dbc%d" % l, [128, 4, NT]))
                gst = ExitStack()
                lay.enter_context(gst)
                gi = sb(gst, "gi%d" % l, [4, S])
                gf = sb(gst, "gf%d" % l, [4, S])
                with ExitStack() as ab:
                    hT = sb(ab, "hT%d" % l, [128, 8, S], BF16)
                    with ExitStack() as ph:
                        xt = [sb(ph, "xt%d" % i, [128, D]) for i in range(3)]
                        junk = [sb(ph, "junkA%d" % i, [128, D]) for i in range(2)]
                        xn = [sb(ph, "xn%d" % i, [128, D], BF16) for i in range(2)]
                        ss = [sb(ph, "ssA%d" % i, [128, 1]) for i in range(3)]
                        rs = [sb(ph, "rsA%d" % i, [128, 1]) for i in range(3)]
                        ptr = [ps(ph, "ptrA%d" % i, [128, D], BF16) for i in range(2)]
                        def a_stage1(t):
                            b = t % 3
                            P.dma("sync", lambda e: e.dma_start(
                                out=xt[b][:], in_=x_src[t * 128:(t + 1) * 128, :]), writes=[("xt", b)])
                            P.op("scalar", lambda e: e.activation(out=junk[b % 2][:], in_=xt[b][:], func=AF.Square),
                                 reads=[("xt", b)], writes=[("junkA", b % 2)])
                            P.op("vector", lambda e: e.reduce_sum(out=ss[b][:], in_=junk[b % 2][:], axis=AX.X),
                                 reads=[("junkA", b % 2)], writes=[("ss", b)])
                            P.op("vector", lambda e: e.tensor_scalar(
                                out=rs[b][:], in0=ss[b][:], scalar1=1.0 / D, scalar2=EPS,
                                op0=ALU.mult, op1=ALU.add), reads=[("ss", b)], writes=[("rs", b)])

                        def a_stage2(t):
                            b = t % 3
                            pb_ = t % 2
                            P.op("scalar", lambda e: e.activation(out=rs[b][:], in_=rs[b][:], func=AF.Sqrt),
                                 reads=[("rs", b)], writes=[("rs", b)])
                            P.op("vector", lambda e: e.reciprocal(out=rs[b][:], in_=rs[b][:]),
                                 reads=[("rs", b)], writes=[("rs", b)])
                            P.op("vector", lambda e: e.tensor_scalar(
                                out=xn[pb_][:], in0=xt[b][:], scalar1=rs[b][:], scalar2=None,
                                op0=ALU.mult), reads=[("xt", b), ("rs", b)], writes=[("xn", pb_)])
                            for c in range(8):
                                P.op("tensor", lambda e, c=c: e.transpose(
                                    ptr[pb_][:, c * 128:(c + 1) * 128], xn[pb_][:, c * 128:(c + 1) * 128],
                                    ident_bf[:]), reads=[("xn", pb_), "ident_bf"], writes=[("ptr", pb_)])
                            for c in range(8):
                                if pb_ == 0:
                                    P.op("scalar", lambda e, c=c: e.activation(
                                        out=hT[:, c, t * 128:(t + 1) * 128], in_=ptr[pb_][:, c * 128:(c + 1) * 128],
                                        func=AF.Identity, bias=modfm[:, l, c:c + 1], scale=gsfm[:, l, c:c + 1]),
                                        reads=[("ptr", pb_)], writes=[("hT", c, t // 4)])
                                else:
                                    P.op("vector", lambda e, c=c: e.tensor_scalar(
                                        out=hT[:, c, t * 128:(t + 1) * 128], in0=ptr[pb_][:, c * 128:(c + 1) * 128],
                                        scalar1=gsfm[:, l, c:c + 1], scalar2=modfm[:, l, c:c + 1],
                                        op0=ALU.mult, op1=ALU.add),
                                        reads=[("ptr", pb_)], writes=[("hT", c, t // 4)])

                        a_stage1(0)
                        for t in range(NT):
                            if t + 1 < NT:
                                a_stage1(t + 1)
                            a_stage2(t)
                        if hT_dump is not None and l == cfg.get("dump_layer", 0):
                            for c in range(8):
                                P.dma("sync", lambda e, c=c: e.dma_start(
                                    out=hT_dump[c * 128:(c + 1) * 128, :], in_=hT[:, c, :]),
                                    reads=[("hT", c, j) for j in range(8)])
                        if finish("A"):
                            return nc

                    with ExitStack() as ph:
                        phase_B(k, ph, l, hT, gi, gf, dict(
                            w_in=w_in, b_i=b_i, b_f=b_f, convw_fm=convw_fm, convb_fm=convb_fm,
                            qkm_T=qkm_T, vm_s=vm_s, gom_s=gom_s, zm_s=zm_s, qa_T=qa_T, ka_T=ka_T,
                            va_s=va_s, za_T=za_T, sgm_T=sgm_T, sga_T=sga_T), sb, ps)
                        if "gates_dump" in cfg.get("dump", ()):
                            gd = nc.dram_tensor("gates_dump", [8, S], F32, kind="ExternalOutput").ap()
                            P.dma("sync", lambda e: e.dma_start(out=gd[0:4, :], in_=gi[:, :]),
                                  reads=[("gate", 0, tt) for tt in range(8)])
                            P.dma("sync", lambda e: e.dma_start(out=gd[4:8, :], in_=gf[:, :]),
                                  reads=[("gate", 1, tt) for tt in range(8)])
                        if finish("B"):
                            return nc
                Tn = dict(ident_f=ident_f, ident_bf=ident_bf, sel4_d=sel4_d, lam_b=lam_b, ag_fm=ag_fm, ka_T=ka_T,
                          qa_T=qa_T, va_s=va_s, za_T=za_T, tbias=tbias, ha_T=ha_T, maskT_d=maskT_d, mg_b=mg_b,
                          qkm_T=qkm_T, vm_s=vm_s, gom_s=gom_s, zm_s=zm_s, hm_T=hm_T, w_pm=w_pm, w_pa=w_pa,
                          w_o=w_o, sgm_T=sgm_T, sga_T=sga_T, fg_b=fg_b, mg_fm=mg_fm)
                with ExitStack() as ph:
                    phase_D0(k, ph, l, gi, gf, cols, Tn, sb, ps)
                    if finish("D0"):
                        return nc
                gst.close()
                if "C" not in cfg.get("skip", ()):
                    with ExitStack() as ph:
                        phase_C(k, ph, l, Tn, sb, ps)
                        if finish("C"):
                            return nc
                if "D1" not in cfg.get("skip", ()):
                    with ExitStack() as ph:
                        phase_D1(k, ph, l, cols, Tn, sb, ps)
                        if finish("D1"):
                            return nc
                with ExitStack() as ph:
                    phase_D2(k, ph, l, last, x_src, x_dst, gateb, Tn, sb, ps)
                    if finish("D2"):
                        return nc
        return nc


def phase_B(k, ph, l, hT, gi, gf, T, sb, ps):
    P = k.P
    w_in = T["w_in"]
    wst = [sb(ph, "wst%d" % i, [128, 8, 512]) for i in range(2)]
    wbf = [sb(ph, "wbf%d" % i, [128, 8, 512], BF16) for i in range(2)]
    ostg = [sb(ph, "ostg%d" % i, [128, S], BF16) for i in range(2)]
    xs = sb(ph, "xs", [128, S + 4])
    acc = sb(ph, "acc", [128, S])
    cw = sb(ph, "cw", [128, 16, 4])
    cb = sb(ph, "cb", [128, 16])
    bi_t = sb(ph, "bi_t", [4, 1])
    bf_t = sb(ph, "bf_t", [4, 1])
    wg = sb(ph, "wg", [128, 8, 8])
    wgb = sb(ph, "wgb", [128, 8, 8], BF16)
    pb = [ps(ph, "pB%d" % i, [128, 512]) for i in range(4)]
    P.dma("sync", lambda e: e.dma_start(out=cw[:], in_=T["convw_fm"][l]), writes=["cw"])
    P.dma("sync", lambda e: e.dma_start(out=cb[:], in_=T["convb_fm"][l]), writes=["cb"])
    P.dma("sync", lambda e: e.dma_start(out=bi_t[:], in_=T["b_i"][l]), writes=["bi_t"])
    P.dma("sync", lambda e: e.dma_start(out=bf_t[:], in_=T["b_f"][l]), writes=["bf_t"])
    P.op("gpsimd", lambda e: e.memset(xs[:, 0:4], 0.0), writes=["xs_halo"])
    hT_all = [("hT", c, j) for c in range(8) for j in range(8)]
    st = {"g": 0, "pb": 0, "os": 0}

    pending = {}

    def load_w(c0, n):
        if c0 in pending:
            return pending.pop(c0)
        g = st["g"] % 2
        st["g"] += 1
        wsrc = w_in[l, :, c0:c0 + n].rearrange("(c p) n -> p c n", p=128)
        P.dma("sync", lambda e: e.dma_start(out=wst[g][:, :, 0:n], in_=wsrc), writes=[("wst", g)])
        castq.append((g, n, 0))
        castq.append((g, n, 1))
        return g

    castq = []

    def do_cast(all_=False):
        while castq:
            g, n, hh = castq.pop(0)
            for q in range(2):
                k0 = 4 * hh + 2 * q
                P.op("scalar", lambda e, g=g, n=n, k0=k0: e.activation(
                    out=wbf[g][:, k0:k0 + 2, 0:n], in_=wst[g][:, k0:k0 + 2, 0:n], func=AF.Copy),
                    reads=[("wst", g)], writes=[("wbf", g, hh)])
            if not all_:
                break

    def next_pb():
        i = st["pb"] % 4
        st["pb"] += 1
        return i

    def conv_quarter(ch, q, o):
        lo = 4 + q * 1024
        rd = [("xs", 2 * q), ("xs", 2 * q + 1), (("xs", 2 * q - 1) if q > 0 else "xs_halo")]
        P.op("vector", lambda e: e.tensor_scalar(
            out=acc[:, q * 1024:(q + 1) * 1024], in0=xs[:, lo:lo + 1024], scalar1=cw[:, ch, 3:4],
            scalar2=cb[:, ch:ch + 1], op0=ALU.mult, op1=ALU.add), reads=rd + ["cw", "cb"], writes=[("acc", q)])
        for tap in (2, 1, 0):
            sh = 3 - tap
            P.op("vector", lambda e, tap=tap, sh=sh: e.scalar_tensor_tensor(
                out=acc[:, q * 1024:(q + 1) * 1024], in0=xs[:, lo - sh:lo - sh + 1024], scalar=cw[:, ch, tap:tap + 1],
                in1=acc[:, q * 1024:(q + 1) * 1024], op0=ALU.mult, op1=ALU.add),
                reads=rd + [("acc", q), "cw"], writes=[("acc", q)])
        siluq.append((evc[0] + 2, lambda: P.op("scalar", lambda e: e.activation(
            out=ostg[o][:, q * 1024:(q + 1) * 1024], in_=acc[:, q * 1024:(q + 1) * 1024], func=AF.Silu),
            reads=[("acc", q)], writes=[("ostg", o, 2 * q), ("ostg", o, 2 * q + 1)])))

    siluq = []
    evc = [0]
    convst = []

    def pop_silu(all_=False):
        while siluq and (all_ or siluq[0][0] <= evc[0]):
            siluq.pop(0)[1]()

    def fm_group(c0, ncols, kind, dst, drow0, convch0=None):
        g = load_w(c0, ncols)
        for j in range(ncols // 128):
            o = st["os"] % 2
            st["os"] += 1
            for tt in range(8):
                pi = next_pb()
                for kc in range(8):
                    P.op("tensor", lambda e, pi=pi, kc=kc, j=j, tt=tt: e.matmul(
                        pb[pi][:, :], lhsT=wbf[g][:, kc, j * 128:(j + 1) * 128],
                        rhs=hT[:, kc, tt * 512:(tt + 1) * 512], start=(kc == 0), stop=(kc == 7)),
                        reads=[("wbf", g, kc // 4), ("hT", kc, tt)], writes=[("pb", pi)])
                sl = slice(tt * 512, (tt + 1) * 512)
                if kind == "conv":
                    P.op("scalar", lambda e, pi=pi, tt=tt: e.activation(
                        out=xs[:, 4 + tt * 512:4 + (tt + 1) * 512], in_=pb[pi][:, :], func=AF.Copy),
                        reads=[("pb", pi)], writes=[("xs", tt)])
                    evc[0] += 1
                    pop_silu()
                    if tt % 2 == 1:
                        conv_quarter(convch0 + j, tt // 2, o)
                elif kind == "copy":
                    if tt % 2 == 0:
                        P.op("scalar", lambda e, pi=pi, sl=sl, o=o: e.activation(
                            out=ostg[o][:, sl], in_=pb[pi][:, :], func=AF.Copy),
                            reads=[("pb", pi)], writes=[("ostg", o, tt)])
                    else:
                        P.op("vector", lambda e, pi=pi, sl=sl, o=o: e.tensor_copy(
                            out=ostg[o][:, sl], in_=pb[pi][:, :]),
                            reads=[("pb", pi)], writes=[("ostg", o, tt)])
                else:
                    fn = AF.Silu if kind == "silu" else AF.Sigmoid
                    P.op("scalar", lambda e, pi=pi, sl=sl, o=o, fn=fn: e.activation(
                        out=ostg[o][:, sl], in_=pb[pi][:, :], func=fn),
                        reads=[("pb", pi)], writes=[("ostg", o, tt)])
            r0 = drow0 + j * 128
            if kind == "conv":
                if convst:
                    pop_prev = convst.pop(0)
                    pop_prev()
                convst.append(lambda o=o, r0=r0: P.dma("sync", lambda e: e.dma_start(
                    out=dst[r0:r0 + 128, :], in_=ostg[o][:, :]), reads=[("ostg", o, tt) for tt in range(8)]))
            else:
                P.dma("sync", lambda e, o=o, r0=r0: e.dma_start(out=dst[r0:r0 + 128, :], in_=ostg[o][:, :]),
                      reads=[("ostg", o, tt) for tt in range(8)])
            if j in (1, 2):
                do_cast()

    def tm_group(c0, kind, dst, dcol0):
        g = load_w(c0, 512)
        for q4 in range(4):
            o = st["os"] % 2
            st["os"] += 1
            for t8 in range(8):
                t = q4 * 8 + t8
                pi = next_pb()
                for kc in range(8):
                    P.op("tensor", lambda e, pi=pi, kc=kc, t=t: e.matmul(
                        pb[pi][:, :], lhsT=hT[:, kc, t * 128:(t + 1) * 128], rhs=wbf[g][:, kc, :],
                        start=(kc == 0), stop=(kc == 7)),
                        reads=[("wbf", g, kc // 4), ("hT", kc, t // 4)], writes=[("pb", pi)])
                sl = slice(t8 * 512, (t8 + 1) * 512)
                if kind == "copy":
                    if t8 % 2 == 0:
                        P.op("scalar", lambda e, pi=pi, sl=sl, o=o: e.activation(
                            out=ostg[o][:, sl], in_=pb[pi][:, :], func=AF.Copy),
                            reads=[("pb", pi)], writes=[("ostg", o, t8)])
                    else:
                        P.op("vector", lambda e, pi=pi, sl=sl, o=o: e.tensor_copy(
                            out=ostg[o][:, sl], in_=pb[pi][:, :]),
                            reads=[("pb", pi)], writes=[("ostg", o, t8)])
                else:
                    fn = AF.Silu if kind == "silu" else AF.Sigmoid
                    P.op("scalar", lambda e, pi=pi, sl=sl, o=o, fn=fn: e.activation(
                        out=ostg[o][:, sl], in_=pb[pi][:, :], func=fn),
                        reads=[("pb", pi)], writes=[("ostg", o, t8)])
            dv = dst[q4 * 1024:(q4 + 1) * 1024, dcol0:dcol0 + 512].rearrange("(t p) c -> p t c", p=128)
            P.dma("sync", lambda e, o=o, dv=dv: e.dma_start(
                out=dv, in_=ostg[o][:, :].rearrange("p (t c) -> p t c", c=512)),
                reads=[("ostg", o, t8) for t8 in range(8)])
            if q4 in (1, 2):
                do_cast()

    only = k.cfg.get("b_only", None)

    def want(name):
        return only is None or name in only

    if want("gates"):
        wsrc = w_in[l, :, C_IM:C_IM + 8].rearrange("(c p) n -> p c n", p=128)
        P.dma("sync", lambda e: e.dma_start(out=wg[:], in_=wsrc), writes=["wg"])
        P.op("gpsimd", lambda e: e.tensor_copy(out=wgb[:], in_=wg[:]), reads=["wg"], writes=["wgb"])
        for which, (gt, bt) in enumerate(((gi, bi_t), (gf, bf_t))):
            for tt in range(8):
                pi = next_pb()
                for kc in range(8):
                    P.op("tensor", lambda e, pi=pi, kc=kc, tt=tt, which=which: e.matmul(
                        pb[pi][0:4, :], lhsT=wgb[:, kc, 4 * which:4 * which + 4],
                        rhs=hT[:, kc, tt * 512:(tt + 1) * 512], start=(kc == 0), stop=(kc == 7)),
                        reads=["wgb", ("hT", kc, tt)], writes=[("pb", pi)])
                P.op("scalar", lambda e, pi=pi, tt=tt, gt=gt, bt=bt: e.activation(
                    out=gt[:, tt * 512:(tt + 1) * 512], in_=pb[pi][0:4, :], func=AF.Identity,
                    bias=bt[:, 0:1], scale=1.0),
                    reads=[("pb", pi), "bi_t", "bf_t"], writes=[("gate", which, tt)])
    groups = []
    if want("qkm"):
        for gidx in range(4):
            groups.append((C_QM + gidx * 512, lambda gidx=gidx: fm_group(
                C_QM + gidx * 512, 512, "conv", T["qkm_T"], gidx * 512, convch0=gidx * 4)))
    if want("vm"):
        for gidx in range(4):
            groups.append((C_VM + gidx * 512, lambda gidx=gidx: tm_group(
                C_VM + gidx * 512, "copy", T["vm_s"], gidx * 512)))
    if want("om"):
        for gidx in range(4):
            groups.append((C_OM + gidx * 512, lambda gidx=gidx: tm_group(
                C_OM + gidx * 512, "sigmoid", T["gom_s"], gidx * 512)))
    if want("zm"):
        for gidx in range(4):
            groups.append((C_ZM + gidx * 512, lambda gidx=gidx: tm_group(
                C_ZM + gidx * 512, "silu", T["zm_s"], gidx * 512)))
    if want("qa"):
        for gidx in range(2):
            groups.append((C_QA + gidx * 512, lambda gidx=gidx: fm_group(
                C_QA + gidx * 512, 512, "copy", T["qa_T"], gidx * 512)))
    if want("ka"):
        for gidx in range(2):
            groups.append((C_KA + gidx * 512, lambda gidx=gidx: fm_group(
                C_KA + gidx * 512, 512, "copy", T["ka_T"], gidx * 512)))
    if want("va"):
        for gidx in range(2):
            groups.append((C_VA + gidx * 512, lambda gidx=gidx: tm_group(
                C_VA + gidx * 512, "copy", T["va_s"], gidx * 512)))
    if want("za"):
        for gidx in range(2):
            groups.append((C_ZA + gidx * 512, lambda gidx=gidx: fm_group(
                C_ZA + gidx * 512, 512, "silu", T["za_T"], gidx * 512)))
    if want("gm"):
        for gidx in range(2):
            groups.append((C_GM + gidx * 512, lambda gidx=gidx: fm_group(
                C_GM + gidx * 512, 512, "sigmoid", T["sgm_T"], gidx * 512)))
    if want("ga"):
        for gidx in range(2):
            groups.append((C_GA + gidx * 512, lambda gidx=gidx: fm_group(
                C_GA + gidx * 512, 512, "sigmoid", T["sga_T"], gidx * 512)))

    if groups:
        pending[groups[0][0]] = load_w(groups[0][0], 512)
        do_cast(all_=True)
    for i, (c0, fn) in enumerate(groups):
        if i + 1 < len(groups):
            nxt = groups[i + 1][0]
            g_ = load_w(nxt, 512)
            pending[nxt] = g_
        fn()
        if i == 3 or i + 1 == len(groups):
            pop_silu(all_=True)
            while convst:
                convst.pop(0)()


def _t5_bucket_np(n):
    n = np.asarray(n)
    nf = np.maximum(n, 1).astype(np.float32)
    large = 16 + (np.log(nf / np.float32(16)) / np.float32(np.log(128 / 16)) * np.float32(16)).astype(np.int32)
    large = np.minimum(large, 31)
    return np.where(n < 16, n, large)


def prep_inputs(inp):
    f32 = np.float32
    A = {k_: np.asarray(v) for k_, v in inp.items()}
    shared = {}
    shared["w_ada"] = np.ascontiguousarray(A["w_ada"], dtype=f32)
    b_ada = A["b_ada"].astype(f32)
    shared["bada_fm"] = np.ascontiguousarray(b_ada[:, :2048].reshape(DEPTH, 16, 128).transpose(0, 2, 1))
    shared["bgate_b"] = np.ascontiguousarray(np.broadcast_to(b_ada[:, None, 2048:], (DEPTH, 128, D)))
    shared["ng_fm"] = np.ascontiguousarray(A["norm_g"].astype(f32).reshape(DEPTH, 8, 128).transpose(0, 2, 1))
    shared["w_in"] = np.ascontiguousarray(A["w_in"], dtype=f32)
    shared["b_i"] = np.ascontiguousarray(A["b_if"][:, :4].astype(f32).reshape(DEPTH, 4, 1))
    shared["b_f"] = np.ascontiguousarray(A["b_if"][:, 4:].astype(f32).reshape(DEPTH, 4, 1))
    shared["convw_fm"] = np.ascontiguousarray(
        A["conv_w"].astype(f32).reshape(DEPTH, 4, 16, 128).transpose(0, 3, 2, 1))
    shared["convb_fm"] = np.ascontiguousarray(A["conv_b"].astype(f32).reshape(DEPTH, 16, 128).transpose(0, 2, 1))
    shared["mg_b"] = np.ascontiguousarray(np.broadcast_to(A["mnorm_g"].astype(f32)[:, None, :], (DEPTH, 128, 2048)))
    shared["mg_fm"] = np.ascontiguousarray(A["mnorm_g"].astype(f32).reshape(DEPTH, 16, 128).transpose(0, 2, 1))
    shared["w_pm"] = np.ascontiguousarray(A["w_proj_m"], dtype=f32)
    shared["w_pa"] = np.ascontiguousarray(A["w_proj_a"], dtype=f32)
    shared["w_o"] = np.ascontiguousarray(A["w_out"], dtype=f32)
    lam4 = np.stack([A["lam_q1"], A["lam_k1"], A["lam_q2"], A["lam_k2"]], axis=1).astype(f32)
    shared["lam_b"] = np.ascontiguousarray(np.broadcast_to(lam4[:, None], (DEPTH, 128, 4, 64)))
    shared["ag_fm"] = np.ascontiguousarray(A["anorm_g"].astype(f32).reshape(DEPTH, 8, 128).transpose(0, 2, 1))
    dlt = np.arange(640)[None, :] - np.arange(128)[:, None]
    bk = _t5_bucket_np(np.maximum(dlt, 0))
    tb = A["rel_bias"].astype(f32)[bk]
    tb = np.where((dlt >= 0)[..., None], tb, f32(-30000.0)).transpose(2, 0, 1)
    shared["tbias"] = np.ascontiguousarray(tb, dtype=f32)
    shared["fg_b"] = np.ascontiguousarray(np.broadcast_to(A["final_g"].astype(f32)[None, :], (128, D)))
    shared["ident_bf"] = np.eye(128, dtype=f32).astype(ml_dtypes.bfloat16)
    shared["ident_f"] = np.eye(128, dtype=f32)
    shared["maskT"] = np.triu(np.ones((128, 128), dtype=f32))
    sel = np.zeros((4, 4, 128), dtype=f32)
    for h in range(4):
        sel[h, h, :] = 1.0
    shared["sel4"] = sel
    maps = []
    x = A["x"].astype(f32)
    c = A["c"].astype(f32)
    for b in range(8):
        m = dict(shared)
        m["x"] = np.ascontiguousarray(x[b])
        m["crep"] = np.ascontiguousarray(np.broadcast_to(c[b].reshape(8, 128).T[:, :, None], (128, 8, 128)))
        maps.append(m)
    return maps


def kernel(**inputs):
    maps = prep_inputs(inputs)
    nc = build({})
    res = run_bass_kernel_spmd(nc, maps, core_ids=list(range(8)))
    return np.stack([np.asarray(r["out"]) for r in res.results], axis=0).astype(np.float32)


def phase_D0(k, ph, l, gi, gf, cols, T, sb, ps):
    P = k.P
    wkcol, wkcol_bf, necol, dbc = cols
    ident_f = T["ident_f"]
    ones4 = sb(ph, "ones4", [4, S])
    e1 = sb(ph, "e1", [4, S])
    Bn = sb(ph, "Bn", [4, S])
    U = sb(ph, "U", [4, S])
    A = sb(ph, "A", [4, S])
    W = sb(ph, "Wr", [4, S])
    N = sb(ph, "Nr", [4, S])
    ash = sb(ph, "ash", [4, 33])
    drow = sb(ph, "drow", [4, 32])
    sel = sb(ph, "sel", [4, 4, 128])
    pcw = ps(ph, "pcw", [128, 512])
    pcn = ps(ph, "pcn", [128, 512])
    pdb = ps(ph, "pdb", [128, 512])
    P.op("gpsimd", lambda e: e.memset(ones4[:, :], 1.0), writes=["ones4"])
    P.dma("sync", lambda e: e.dma_start(out=sel[:], in_=T["sel4_d"]), writes=["sel"])
    P.op("scalar", lambda e: e.activation(out=e1[:, :], in_=gf[:, :], func=AF.Exp, scale=-1.0), writes=["e1"])
    P.op("scalar", lambda e: e.activation(out=e1[:, :], in_=e1[:, :], func=AF.Ln, bias=1.0),
         reads=["e1"], writes=["e1"])
    P.op("vector", lambda e: e.tensor_tensor_scan(out=Bn[:, :], data0=ones4[:, :], data1=e1[:, :],
                                                  initial=0.0, op0=ALU.mult, op1=ALU.add),
         reads=["ones4", "e1"], writes=["Bn"])
    P.op("vector", lambda e: e.tensor_tensor(out=U[:, :], in0=gi[:, :], in1=Bn[:, :], op=ALU.add),
         reads=["Bn"], writes=["U"])
    P.op("vector", lambda e: e.tensor_tensor_scan(out=A[:, :], data0=ones4[:, :], data1=U[:, :],
                                                  initial=0.0, op0=ALU.mult, op1=ALU.max),
         reads=["ones4", "U"], writes=["A"])
    A3 = A[:, :].rearrange("p (c t) -> p c t", t=128)
    alb = A3[:, :, 127:128].to_broadcast([4, 32, 128])
    P.op("vector", lambda e: e.tensor_tensor(
        out=W[:, :].rearrange("p (c t) -> p c t", t=128), in0=U[:, :].rearrange("p (c t) -> p c t", t=128),
        in1=alb, op=ALU.subtract), reads=["U", "A"], writes=["W"])
    P.op("vector", lambda e: e.tensor_tensor(
        out=N[:, :].rearrange("p (c t) -> p c t", t=128), in0=Bn[:, :].rearrange("p (c t) -> p c t", t=128),
        in1=alb, op=ALU.subtract), reads=["Bn", "A"], writes=["N"])
    P.op("scalar", lambda e: e.activation(out=W[:, :], in_=W[:, :], func=AF.Exp), reads=["W"], writes=["W"])
    P.op("scalar", lambda e: e.activation(out=N[:, :], in_=N[:, :], func=AF.Exp), reads=["N"], writes=["N"])
    P.op("vector", lambda e: e.memset(ash[:, 0:1], 0.0), writes=["ash0"])
    P.op("vector", lambda e: e.tensor_copy(out=ash[:, 1:33], in_=A3[:, :, 127]), reads=["A"], writes=["ash1"])
    P.op("vector", lambda e: e.tensor_tensor(out=drow[:, :], in0=ash[:, 0:32], in1=ash[:, 1:33], op=ALU.subtract),
         reads=["ash0", "ash1"], writes=["drow"])
    P.op("scalar", lambda e: e.activation(out=drow[:, :], in_=drow[:, :], func=AF.Exp),
         reads=["drow"], writes=["drow"])
    for c in range(NT):
        P.op("tensor", lambda e, c=c: e.transpose(pcw[:, 4 * c:4 * c + 4], W[0:4, c * 128:(c + 1) * 128],
                                                  ident_f[0:4, 0:4]), reads=["W"], writes=["pcw"])
    for c in range(NT):
        P.op("tensor", lambda e, c=c: e.transpose(pcn[:, 4 * c:4 * c + 4], N[0:4, c * 128:(c + 1) * 128],
                                                  ident_f[0:4, 0:4]), reads=["N"], writes=["pcn"])
    P.op("vector", lambda e: e.tensor_scalar(out=wkcol[:, :, :].rearrange("p c h -> p (c h)"), in0=pcw[:, 0:128],
                                             scalar1=0.0625, scalar2=None, op0=ALU.mult),
         reads=["pcw"], writes=["wkcol"])
    P.op("vector", lambda e: e.tensor_copy(out=wkcol_bf[:, :, :].rearrange("p c h -> p (c h)"), in_=pcw[:, 0:128]),
         reads=["pcw"], writes=["wkcol_bf"])
    P.op("vector", lambda e: e.tensor_copy(out=necol[:, :, :].rearrange("p c h -> p (c h)"), in_=pcn[:, 0:128]),
         reads=["pcn"], writes=["necol"])
    for h in range(4):
        P.op("tensor", lambda e, h=h: e.matmul(pdb[:, h * 32:(h + 1) * 32], lhsT=sel[:, h, :], rhs=drow[:, :],
                                               start=True, stop=True), reads=["sel", "drow"], writes=["pdb"])
    P.op("vector", lambda e: e.tensor_copy(out=dbc[:, :, :].rearrange("p h c -> p (h c)"), in_=pdb[:, 0:128]),
         reads=["pdb"], writes=["dbc"])


def phase_C(k, ph, l, T, sb, ps):
    import math
    P = k.P
    lam_init = 0.8 - 0.6 * math.exp(-0.3 * l)
    ones_bf = sb(ph, "onesbf", [128, 128], BF16)
    ones_f = sb(ph, "onesf", [128, 128])
    lamt = sb(ph, "lamt", [128, 4, 64])
    lprod = sb(ph, "lprod", [128, 2, 64])
    lsum = sb(ph, "lsum", [128, 2])
    neglam = sb(ph, "neglam", [128, 1])
    epsC = sb(ph, "epsC", [128, 1])
    KT = [sb(ph, "KT%d" % i, [128, 2, S], BF16) for i in range(2)]
    TB8 = [sb(ph, "TB8_%d" % i, [128, 640], BF16) for i in range(2)]
    ident_bf = T["ident_bf"]
    QT = [sb(ph, "QT%d" % i, [128, S], BF16) for i in range(2)]
    ZA = [sb(ph, "ZA%d" % i, [128, S], BF16) for i in range(2)]
    V = [sb(ph, "V%d" % i, [128, NT, 128], BF16) for i in range(2)]
    TB = [sb(ph, "TB%d" % i, [128, 640]) for i in range(2)]
    cst = [sb(ph, "cst%d" % i, [128, 1]) for i in range(2)]
    HA = [sb(ph, "HA%d" % i, [128, S], BF16) for i in range(2)]
    PT = [sb(ph, "PT%d" % i, [128, 2, 512], BF16) for i in range(2)]
    stmp = [sb(ph, "stmp%d" % i, [128, 2, 512]) for i in range(2)]
    oc = [[sb(ph, "oc%d_%d" % (i, m), [128, 512]) for m in range(2)] for i in range(3)]
    mc = [[sb(ph, "mc%d_%d" % (i, m), [128, 512]) for m in range(2)] for i in range(3)]
    o_t = sb(ph, "o_t", [128, 512])
    sq = sb(ph, "sq_t", [128, 512])
    rstd = sb(ph, "rstdC", [128, 512])
    pS = [ps(ph, "pS%d" % i, [128, 1024]) for i in range(2)]
    pO = [ps(ph, "pO%d" % i, [128, 512]) for i in range(2)]
    pM = [ps(ph, "pM%d" % i, [128, 512]) for i in range(2)]

    P.op("gpsimd", lambda e: e.memset(ones_bf[:, :], 1.0), writes=["ones_bf"])
    P.op("gpsimd", lambda e: e.memset(ones_f[:, :], 1.0), writes=["ones_f"])
    P.op("gpsimd", lambda e: e.memset(epsC[:, :], EPS), writes=["epsC"])
    for b_ in range(2):
        P.op("vector", lambda e, b_=b_: e.memset(KT[b_][:, :, :].rearrange("p m s -> p (m s)"), 0.0),
             writes=[("KT", b_)])
    P.dma("sync", lambda e: e.dma_start(out=lamt[:], in_=T["lam_b"][l]), writes=["lamt"])
    P.op("vector", lambda e: e.tensor_tensor(out=lprod[:, :, :], in0=lamt[:, 0:4:2, :], in1=lamt[:, 1:4:2, :],
                                             op=ALU.mult), reads=["lamt"], writes=["lprod"])
    P.op("vector", lambda e: e.reduce_sum(out=lsum[:, :], in_=lprod[:, :, :], axis=AX.X),
         reads=["lprod"], writes=["lsum"])
    P.op("scalar", lambda e: e.activation(out=lsum[:, :], in_=lsum[:, :], func=AF.Exp),
         reads=["lsum"], writes=["lsum"])
    P.op("vector", lambda e: e.tensor_tensor(out=neglam[:, :], in0=lsum[:, 1:2], in1=lsum[:, 0:1],
                                             op=ALU.subtract), reads=["lsum"], writes=["neglam"])
    P.op("vector", lambda e: e.tensor_scalar(out=neglam[:, :], in0=neglam[:, :], scalar1=-lam_init, scalar2=None,
                                             op0=ALU.add), reads=["neglam"], writes=["neglam"])

    heads = k.cfg.get("c_heads", list(range(8)))
    jmax = k.cfg.get("c_jmax", 8)

    def load_head(i, h):
        b = i % 2
        r = slice(h * 128, (h + 1) * 128)
        for m in range(2):
            P.dma("sync", lambda e, m=m: e.dma_start(
                out=KT[b][m * 64:(m + 1) * 64, m, :], in_=T["ka_T"][h * 128 + m * 64:h * 128 + (m + 1) * 64, :]),
                writes=[("KT", b)])
        P.dma("sync", lambda e: e.dma_start(out=QT[b][:, :], in_=T["qa_T"][r, :]), writes=[("QT", b)])
        P.dma("sync", lambda e: e.dma_start(out=TB[b][:, :], in_=T["tbias"][h]), writes=[("TB", b)])
        P.dma("sync", lambda e: e.dma_start(
            out=V[b][:, :, :], in_=T["va_s"][:, r].rearrange("(t p) c -> p t c", p=128)), writes=[("V", b)])
        P.dma("sync", lambda e: e.dma_start(out=ZA[b][:, :], in_=T["za_T"][r, :]), writes=[("ZA", b)])
        P.op("vector", lambda e: e.tensor_copy(out=cst[b][:, :], in_=TB[b][:, 639:640]),
             reads=[("TB", b)], writes=[("cst", b)])
        P.op("vector", lambda e: e.tensor_scalar(out=TB8[b][:, :], in0=TB[b][:, :], scalar1=cst[b][:, 0:1],
                                                 scalar2=8.0, op0=ALU.subtract, op1=ALU.mult),
             reads=[("TB", b), ("cst", b)], writes=[("TB8", b)])

    pairs = []
    for i, h in enumerate(heads):
        for j in range(jmax):
            nkb = 4 * j + 4
            for kb in range(nkb):
                pairs.append(dict(i=i, h=h, j=j, kb=kb, first=(kb == 0), last=(kb == nkb - 1)))
    n = len(pairs)
    ring = [0]
    deferred = []
    tilecnt = [0]

    def geom(s):
        rel = s["kb"] - 4 * s["j"]
        return rel, max(0, rel) * 128

    def rec_qk(pi):
        s = pairs[pi]
        b = s["i"] % 2
        rel, c0 = geom(s)
        buf = pi % 2
        s["buf"] = buf
        j, kb = s["j"], s["kb"]
        if rel >= -1:
            x0 = 128 if rel == -1 else 0
            nn = 512 - c0
            for m in range(2):
                P.op("tensor", lambda e, m=m: e.matmul(
                    pS[buf][:, m * 512 + c0:(m + 1) * 512], lhsT=KT[b][:, m, kb * 128:(kb + 1) * 128],
                    rhs=QT[b][:, j * 512 + c0:(j + 1) * 512], start=True, stop=False),
                    reads=[("KT", b), ("QT", b)], writes=[("pS", buf)])
                nb = 128 if rel == -1 else min(nn, 256)
                P.op("tensor", lambda e, m=m, nb=nb: e.matmul(
                    pS[buf][:, m * 512 + c0:m * 512 + c0 + nb], lhsT=ident_bf[:, :], rhs=TB8[b][:, x0:x0 + nb],
                    start=False, stop=True),
                    reads=[("TB8", b)], writes=[("pS", buf)])
        else:
            for m in range(2):
                P.op("tensor", lambda e, m=m: e.matmul(
                    pS[buf][:, m * 512 + c0:(m + 1) * 512],
                    lhsT=KT[b][m * 64:(m + 1) * 64, m, kb * 128:(kb + 1) * 128],
                    rhs=QT[b][m * 64:(m + 1) * 64, j * 512 + c0:(j + 1) * 512], start=True, stop=True),
                    reads=[("KT", b), ("QT", b)], writes=[("pS", buf)])

    def rec_exp(pi):
        s = pairs[pi]
        b = s["i"] % 2
        rel, c0 = geom(s)
        buf = s["buf"]
        pt = pi % 2
        nn = 512 - c0
        psv = pS[buf][:, :].rearrange("p (m q) -> p m q", m=2)[:, :, c0:512]
        P.op("scalar", lambda e: e.activation(out=PT[pt][:, :, c0:512], in_=psv, func=AF.Exp, scale=0.125),
             reads=[("pS", buf)], writes=[("PT", pt)])

    def rec_av(pi):
        s = pairs[pi]
        b = s["i"] % 2
        rel, c0 = geom(s)
        pt = pi % 2
        kb = s["kb"]
        for m in range(2):
            P.op("tensor", lambda e, m=m: e.matmul(pO[m][:, c0:512], lhsT=V[b][:, kb, :], rhs=PT[pt][:, m, c0:512],
                                                   start=s["first"], stop=s["last"]),
                 reads=[("V", b), ("PT", pt)], writes=[("pO", m)])
        for m in range(2):
            P.op("tensor", lambda e, m=m: e.matmul(pM[m][:, c0:512], lhsT=ones_bf[:, :], rhs=PT[pt][:, m, c0:512],
                                                   start=s["first"], stop=s["last"]),
                 reads=["ones_bf", ("PT", pt)], writes=[("pM", m)])

    def epi1(s):
        tb = tilecnt[0] % 3
        tilecnt[0] += 1
        s["tb"] = tb
        for m in range(2):
            P.op("vector", lambda e, m=m: e.tensor_copy(out=oc[tb][m][:, :], in_=pO[m][:, :]),
                 reads=[("pO", m)], writes=[("oc", tb, m)])
            P.op("vector", lambda e, m=m: e.tensor_copy(out=mc[tb][m][:, :], in_=pM[m][:, :]),
                 reads=[("pM", m)], writes=[("mc", tb, m, q) for q in range(4)])

    def epi_pieces(s):
        b = s["i"] % 2
        h, j, tb = s["h"], s["j"], s["tb"]
        pcs = []
        for m in range(2):
            for q in range(4):
                qs = slice(q * 128, (q + 1) * 128)
                pcs.append(lambda buf, m=m, q=q, qs=qs: P.op(
                    "vector", lambda e: e.reciprocal(out=mc[tb][m][:, qs], in_=mc[tb][m][:, qs]),
                    reads=[("mc", tb, m, q)], writes=[("mc", tb, m, q)]))
        pcs.append(lambda buf: P.op("vector", lambda e: e.tensor_tensor(
            out=oc[tb][0][:, :], in0=oc[tb][0][:, :], in1=mc[tb][0][:, :], op=ALU.mult),
            reads=[("oc", tb, 0)] + [("mc", tb, 0, q) for q in range(4)], writes=[("oc", tb, 0)]))

        pcs.append(lambda buf: P.op("vector", lambda e: e.scalar_tensor_tensor(
            out=oc[tb][1][:, :], in0=oc[tb][1][:, :], scalar=neglam[:, 0:1], in1=mc[tb][1][:, :],
            op0=ALU.mult, op1=ALU.mult),
            reads=[("oc", tb, 1), "neglam"] + [("mc", tb, 1, q) for q in range(4)], writes=[("oc", tb, 1)]))

        def pc_o(buf):
            P.op("gpsimd", lambda e: e.tensor_tensor(out=o_t[:, :], in0=oc[tb][0][:, :], in1=oc[tb][1][:, :],
                                                     op=ALU.add),
                 reads=[("oc", tb, 0), ("oc", tb, 1)], writes=["o_t"])
            P.op("gpsimd", lambda e: e.tensor_tensor(out=sq[:, :], in0=o_t[:, :], in1=o_t[:, :], op=ALU.mult),
                 reads=["o_t"], writes=["sq"])
        pcs.append(pc_o)
        pcs.append(lambda buf: None)

        def pc_ssq(buf):
            P.op("tensor", lambda e: e.matmul(pS[buf][:, 0:512], lhsT=ones_f[:, :], rhs=sq[:, :],
                                              start=True, stop=True),
                 reads=["ones_f", "sq"], writes=[("pS", buf)])
            P.op("vector", lambda e: e.tensor_scalar(out=rstd[:, :], in0=pS[buf][:, 0:512], scalar1=1.0 / 128,
                                                     scalar2=EPS, op0=ALU.mult, op1=ALU.add),
                 reads=[("pS", buf)], writes=["rstd"])
        pcs.append(pc_ssq)
        pcs.append(lambda buf: None)

        def pc_rs(buf):
            P.op("scalar", lambda e: e.activation(out=rstd[:, :], in_=rstd[:, :], func=AF.Ln),
                 reads=["rstd"], writes=["rstd"])
            P.op("scalar", lambda e: e.activation(out=rstd[:, :], in_=rstd[:, :], func=AF.Exp, scale=-0.5),
                 reads=["rstd"], writes=["rstd"])
        pcs.append(pc_rs)
        pcs.append(lambda buf: None)
        pcs.append(lambda buf: P.op("vector", lambda e: e.tensor_tensor(
            out=rstd[:, :], in0=rstd[:, :], in1=o_t[:, :], op=ALU.mult),
            reads=["rstd", "o_t"], writes=["rstd"]))

        def pc_fin(buf):
            P.op("vector", lambda e: e.tensor_tensor(
                out=HA[b][:, j * 512:(j + 1) * 512], in0=rstd[:, :], in1=ZA[b][:, j * 512:(j + 1) * 512],
                op=ALU.mult), reads=["rstd", ("ZA", b)], writes=[("HA", b, j)])
            if j == jmax - 1:
                P.dma("sync", lambda e: e.dma_start(out=T["ha_T"][h * 128:(h + 1) * 128, 0:jmax * 512],
                                                    in_=HA[b][:, 0:jmax * 512]),
                      reads=[("HA", b, jj) for jj in range(jmax)])
        pcs.append(pc_fin)
        return pcs

    fifo = []
    load_head(0, heads[0])
    rec_qk(0)
    for pi in range(n):
        s = pairs[pi]
        if s["kb"] == 0 and s["j"] == 0 and s["i"] + 1 < len(heads):
            while fifo:
                fifo.pop(0)(1 - s["buf"])
            load_head(s["i"] + 1, heads[s["i"] + 1])
        if pi + 1 < n:
            rec_qk(pi + 1)
        rec_exp(pi)
        rec_av(pi)
        if s["last"]:
            epi1(s)
            fifo.extend(epi_pieces(s))
        for _ in range(2 if len(fifo) > 10 else 1):
            if fifo:
                fifo.pop(0)(s["buf"])
    while fifo:
        fifo.pop(0)(pairs[n - 1]["buf"])


def phase_D1(k, ph, l, cols, T, sb, ps):
    P = k.P
    wkcol, wkcol_bf, necol, dbc = cols
    ident_bf = T["ident_bf"]
    maskT = sb(ph, "maskT", [128, 128])
    C32 = sb(ph, "C32", [128, 4, 2, 512])
    Cbf = sb(ph, "Cbf", [128, 4, 2, 512], BF16)
    n32 = sb(ph, "n32", [128, 4, 2])
    nbf = sb(ph, "nbf", [128, 4, 2], BF16)
    qk = [sb(ph, "qk%d" % i, [128, 16, 512], BF16) for i in range(2)]
    vt = [sb(ph, "vt%d" % i, [128, 2048], BF16) for i in range(2)]
    gt = [sb(ph, "gt%d" % i, [128, 2048], BF16) for i in range(2)]
    zt = [sb(ph, "zt%d" % i, [128, 2048], BF16) for i in range(2)]
    hmT = [sb(ph, "hmT%d" % i, [128, 16, 512], BF16) for i in range(2)]
    sTm = [sb(ph, "sTm%d" % i, [128, 128], BF16) for i in range(2)]
    ktm = [sb(ph, "ktm%d" % i, [128, 256], BF16) for i in range(2)]
    hg = [sb(ph, "hg%d" % i, [128, 512]) for i in range(2)]
    hm = [sb(ph, "hm%d" % i, [128, 512], BF16) for i in range(2)]
    junk = sb(ph, "junkD", [128, 512])
    rden = [sb(ph, "rden%d" % i, [128, 1]) for i in range(2)]
    ssq = [sb(ph, "ssqD%d" % i, [128, 1]) for i in range(2)]
    rsd = [sb(ph, "rsdD%d" % i, [128, 1]) for i in range(2)]
    p_sT = ps(ph, "p_sT", [128, 512])
    p_kt = ps(ph, "p_kt", [128, 1024], BF16)
    p_num = ps(ph, "p_num", [128, 512])
    p_den = ps(ph, "p_den", [128, 512])
    p_pn = ps(ph, "p_pn", [128, 512])
    p_C = ps(ph, "p_C", [128, 1024])
    onesc = sb(ph, "onesc", [128, 2], BF16)
    P.op("gpsimd", lambda e: e.memset(onesc[:, :], 1.0), writes=["onesc"])
    epsD = sb(ph, "epsD", [128, 1])
    P.op("gpsimd", lambda e: e.memset(epsD[:, :], EPS), writes=["epsD"])
    p_hm = ps(ph, "p_hm", [128, 1024], BF16)

    P.dma("sync", lambda e: e.dma_start(out=maskT[:], in_=T["maskT_d"]), writes=["maskT"])
    P.op("gpsimd", lambda e: e.memset(C32[:, :, :, :].rearrange("p h d v -> p (h d v)"), 0.0),
         writes=[("C32", h) for h in range(4)])
    P.op("gpsimd", lambda e: e.memset(Cbf[:, :, :, :].rearrange("p h d v -> p (h d v)"), 0.0),
         writes=[("Cbf", h) for h in range(4)])
    P.op("gpsimd", lambda e: e.memset(n32[:, :, :].rearrange("p h d -> p (h d)"), 0.0),
         writes=[("n32", h) for h in range(4)])
    P.op("gpsimd", lambda e: e.memset(nbf[:, :, :].rearrange("p h d -> p (h d)"), 0.0),
         writes=[("nbf", h) for h in range(4)])

    nchunks = k.cfg.get("d1_chunks", NT)

    def load_super(J):
        b = J % 2
        P.dma("sync", lambda e: e.dma_start(
            out=qk[b][:, :, :], in_=T["qkm_T"][:, J * 512:(J + 1) * 512].rearrange("(c p) t -> p c t", p=128)),
            writes=[("qk", b)])

    def load_chunk(c):
        b = c % 2
        r = slice(c * 128, (c + 1) * 128)
        P.dma("sync", lambda e: e.dma_start(out=vt[b][:, :], in_=T["vm_s"][r, :]), writes=[("vt", b)])
        P.dma("sync", lambda e: e.dma_start(out=gt[b][:, :], in_=T["gom_s"][r, :]), writes=[("gt", b)])
        P.dma("sync", lambda e: e.dma_start(out=zt[b][:, :], in_=T["zm_s"][r, :]), writes=[("zt", b)])

    pend = []

    def hm_transposes(c, h, w):
        J, tl = c // 4, c % 4
        sbi = J % 2
        tsl = slice(tl * 128, (tl + 1) * 128)
        for vc in range(4):
            P.op("tensor", lambda e, vc=vc: e.transpose(
                p_hm[:, vc * 128:(vc + 1) * 128], hm[w][:, vc * 128:(vc + 1) * 128], ident_bf[:]),
                reads=[("hm", w)], writes=["p_hm"])
        P.op("scalar", lambda e: e.activation(
            out=hmT[sbi][:, h * 4:(h + 1) * 4, tsl],
            in_=p_hm[:, 0:512].rearrange("p (v t) -> p v t", t=128), func=AF.Copy),
            reads=["p_hm"], writes=[("hmT", sbi)])

    def do_step(part, c, h, w, J, tl, qb, cb, tsl):
        hs = slice(h * 512, (h + 1) * 512)

        def qT(d):
            return qk[qb][:, 2 * h + d, tsl]

        def kT(d):
            return qk[qb][:, 8 + 2 * h + d, tsl]

        if part == "A":
            do_step_A(c, h, w, qb, qT, kT)
        else:
            do_step_B(c, h, w, cb, qb, hs, qT)

    def do_step_A(c, h, w, qb, qT, kT):
        for d in range(2):
            P.op("tensor", lambda e, d=d: e.matmul(p_sT[:, 0:128], lhsT=kT(d), rhs=qT(d),
                                                  start=(d == 0), stop=(d == 1)),
                 reads=[("qk", qb)], writes=["p_sT"])
        for d in range(2):
            P.op("tensor", lambda e, d=d: e.transpose(p_kt[:, d * 128:(d + 1) * 128], kT(d), ident_bf[:]),
                 reads=[("qk", qb)], writes=["p_kt"])
        P.op("vector", lambda e, w=w: e.scalar_tensor_tensor(
            out=sTm[w][:, :], in0=p_sT[:, 0:128], scalar=wkcol[:, c, h:h + 1], in1=maskT[:, :],
            op0=ALU.mult, op1=ALU.mult), reads=["p_sT", "maskT"], writes=[("sTm", w)])
        P.op("scalar", lambda e, w=w: e.activation(out=ktm[w][:, :], in_=p_kt[:, 0:256], func=AF.Identity,
                                                   scale=wkcol[:, c, h:h + 1]), reads=["p_kt"], writes=[("ktm", w)])

    def do_step_B(c, h, w, cb, qb, hs, qT):
        P.op("tensor", lambda e, h=h: e.matmul(p_num[:, :], lhsT=qT(0), rhs=Cbf[:, h, 0, :], start=True, stop=False),
             reads=[("qk", qb), ("Cbf", h)], writes=["p_num"])
        P.op("tensor", lambda e, h=h: e.matmul(p_num[:, :], lhsT=qT(1), rhs=Cbf[:, h, 1, :], start=False, stop=False),
             reads=[("qk", qb), ("Cbf", h)], writes=["p_num"])
        P.op("tensor", lambda e, w=w: e.matmul(p_num[:, :], lhsT=sTm[w][:, :], rhs=vt[cb][:, hs], start=False, stop=True),
             reads=[("sTm", w), ("vt", cb)], writes=["p_num"])
        P.op("tensor", lambda e, h=h: e.matmul(p_den[:, 0:1], lhsT=qT(0), rhs=nbf[:, h, 0:1], start=True, stop=False),
             reads=[("qk", qb), ("nbf", h)], writes=["p_den"])
        P.op("tensor", lambda e, h=h: e.matmul(p_den[:, 0:1], lhsT=qT(1), rhs=nbf[:, h, 1:2], start=False, stop=False),
             reads=[("qk", qb), ("nbf", h)], writes=["p_den"])
        P.op("tensor", lambda e, w=w: e.matmul(p_den[:, 0:1], lhsT=sTm[w][:, :], rhs=onesc[:, 0:1],
                                               start=False, stop=True),
             reads=[("sTm", w), "onesc"], writes=["p_den"])
        if c < NT - 1:
            for d in range(2):
                P.op("tensor", lambda e, w=w, d=d: e.matmul(
                    p_C[:, d * 512:(d + 1) * 512], lhsT=ktm[w][:, d * 128:(d + 1) * 128], rhs=vt[cb][:, hs],
                    start=True, stop=True),
                    reads=[("ktm", w), ("vt", cb)], writes=["p_C"])
            for d in range(2):
                P.op("tensor", lambda e, w=w, d=d: e.matmul(
                    p_pn[:, d:d + 1], lhsT=ktm[w][:, d * 128:(d + 1) * 128], rhs=onesc[:, 0:1],
                    start=True, stop=True), reads=[("ktm", w), "onesc"], writes=["p_pn"])
        while pend:
            hm_transposes(*pend.pop(0))
        P.op("vector", lambda e, w=w: e.tensor_scalar(
            out=rden[w][:, :], in0=p_den[:, 0:1], scalar1=-1.0, scalar2=None, op0=ALU.mult),
            reads=["p_den"], writes=[("rden", w)])
        P.op("vector", lambda e, w=w: e.tensor_tensor(
            out=rden[w][:, :], in0=p_den[:, 0:1], in1=rden[w][:, :], op=ALU.max),
            reads=["p_den", ("rden", w)], writes=[("rden", w)])
        P.op("vector", lambda e, w=w, c=c, h=h: e.tensor_scalar(
            out=rden[w][:, :], in0=rden[w][:, :], scalar1=necol[:, c, h:h + 1], scalar2=None, op0=ALU.max),
            reads=[("rden", w)], writes=[("rden", w)])
        P.op("vector", lambda e, w=w: e.reciprocal(out=rden[w][:, :], in_=rden[w][:, :]),
             reads=[("rden", w)], writes=[("rden", w)])
        P.op("vector", lambda e, w=w, hs=hs: e.scalar_tensor_tensor(
            out=hg[w][:, :], in0=p_num[:, :], scalar=rden[w][:, 0:1], in1=gt[cb][:, hs],
            op0=ALU.mult, op1=ALU.mult), reads=["p_num", ("rden", w), ("gt", cb)], writes=[("hg", w)])
        P.op("scalar", lambda e, w=w: e.activation(out=junk[:, :], in_=hg[w][:, :], func=AF.Square),
             reads=[("hg", w)], writes=["junkD"])
        P.op("vector", lambda e, w=w: e.reduce_sum(out=ssq[w][:, :], in_=junk[:, :], axis=AX.X),
             reads=["junkD"], writes=[("ssq", w)])
        P.op("scalar", lambda e, w=w: e.activation(out=rsd[w][:, :], in_=ssq[w][:, :], func=AF.Sqrt,
                                                   scale=1.0 / 512, bias=epsD[:, 0:1]),
             reads=[("ssq", w), "epsD"], writes=[("rsd", w)])
        P.op("vector", lambda e, w=w: e.reciprocal(out=rsd[w][:, :], in_=rsd[w][:, :]),
             reads=[("rsd", w)], writes=[("rsd", w)])
        P.op("vector", lambda e, w=w: e.scalar_tensor_tensor(
            out=hm[w][:, :], in0=hg[w][:, :], scalar=rsd[w][:, 0:1], in1=zt[cb][:, hs],
            op0=ALU.mult, op1=ALU.mult), reads=[("hg", w), ("rsd", w), ("zt", cb)], writes=[("hm", w)])
        pend.append((c, h, w))
        if c < NT - 1:
            P.op("vector", lambda e: e.scalar_tensor_tensor(
                out=C32[:, h, :, :], in0=C32[:, h, :, :], scalar=dbc[:, h, c:c + 1],
                in1=p_C[:, :].rearrange("p (d v) -> p d v", d=2), op0=ALU.mult, op1=ALU.add),
                reads=["p_C", ("C32", h)], writes=[("C32", h)])
            P.op("scalar", lambda e: e.activation(
                out=Cbf[:, h, :, :], in_=C32[:, h, :, :], func=AF.Identity, scale=dbc[:, h, c + 1:c + 2]),
                reads=[("C32", h)], writes=[("Cbf", h)])
            P.op("vector", lambda e: e.scalar_tensor_tensor(
                out=n32[:, h, :], in0=n32[:, h, :], scalar=dbc[:, h, c:c + 1], in1=p_pn[:, 0:2],
                op0=ALU.mult, op1=ALU.add), reads=["p_pn", ("n32", h)], writes=[("n32", h)])
            P.op("scalar", lambda e: e.activation(
                out=nbf[:, h, :], in_=n32[:, h, :], func=AF.Identity, scale=dbc[:, h, c + 1:c + 2]),
                reads=[("n32", h)], writes=[("nbf", h)])

    load_super(0)
    load_chunk(0)
    def step_args(st):
        c, h = st // 4, st % 4
        J, tl = c // 4, c % 4
        return (c, h, st % 2, J, tl, J % 2, c % 2, slice(tl * 128, (tl + 1) * 128))

    nsteps = nchunks * 4
    do_step("A", *step_args(0))
    for c in range(nchunks):
        J, tl = c // 4, c % 4
        if tl == 0 and (c + 4) < nchunks:
            load_super(J + 1)
        if c + 1 < nchunks:
            load_chunk(c + 1)
        for h in range(4):
            st = c * 4 + h
            if st + 1 < nsteps:
                do_step("A", *step_args(st + 1))
            do_step("B", *step_args(st))
        if tl == 3 or c == nchunks - 1:
            while pend:
                hm_transposes(*pend.pop(0))
            sbi = J % 2
            P.dma("sync", lambda e, J=J, sbi=sbi: e.dma_start(
                out=T["hm_T"][:, J * 512:(J + 1) * 512].rearrange("(c p) t -> p c t", p=128), in_=hmT[sbi][:, :, :]),
                reads=[("hmT", sbi)])


def phase_D2(k, ph, l, last, x_src, x_dst, gateb, T, sb, ps):
    P = k.P
    wpm = sb(ph, "wpm", [128, 16, 1024], BF16)
    wpa = sb(ph, "wpa", [128, 8, 1024], BF16)
    wo = sb(ph, "wo", [128, 8, 1024], BF16)
    wst = [sb(ph, "wstD%d" % i, [128, 8, 512]) for i in range(2)]
    HMT = sb(ph, "HMT", [128, 16, 512], BF16)
    HAT = sb(ph, "HAT", [128, 8, 512], BF16)
    SGM = sb(ph, "SGM", [128, 8, 512], BF16)
    SGA = sb(ph, "SGA", [128, 8, 512], BF16)
    YT = sb(ph, "YT", [128, 8, 512], BF16)
    t1 = [sb(ph, "t1_%d" % i, [128, 512]) for i in range(2)]
    t2 = [sb(ph, "t2_%d" % i, [128, 512]) for i in range(2)]
    xt = [sb(ph, "xtD%d" % i, [128, D]) for i in range(2)]
    xn = [sb(ph, "xnD%d" % i, [128, D]) for i in range(2)]
    ot = [sb(ph, "otD%d" % i, [128, D]) for i in range(2)]
    fg = sb(ph, "fg", [128, D])
    junk = sb(ph, "junkE", [128, D])
    ss = [sb(ph, "ssE%d" % i, [128, 1]) for i in range(2)]
    rs = [sb(ph, "rsE%d" % i, [128, 1]) for i in range(2)]
    p_ym = [ps(ph, "p_ym%d" % i, [128, 512]) for i in range(2)]
    p_ya = [ps(ph, "p_ya%d" % i, [128, 512]) for i in range(2)]
    p_wo = [ps(ph, "p_wo%d" % i, [128, 512]) for i in range(4)]

    if last:
        P.dma("sync", lambda e: e.dma_start(out=fg[:], in_=T["fg_b"]), writes=["fg"])
    rounds = []
    for rh in range(2):
        for ch in range(2):
            rounds.append((T["w_pm"][l, rh * 1024:(rh + 1) * 1024, ch * 512:(ch + 1) * 512],
                           wpm[:, rh * 8:(rh + 1) * 8, ch * 512:(ch + 1) * 512], ("wpm", rh, ch)))
    for ch in range(2):
        rounds.append((T["w_pa"][l, :, ch * 512:(ch + 1) * 512], wpa[:, :, ch * 512:(ch + 1) * 512], ("wpa", ch)))
    for ch in range(2):
        rounds.append((T["w_o"][l, :, ch * 512:(ch + 1) * 512], wo[:, :, ch * 512:(ch + 1) * 512], ("wo", ch)))
    import math
    lam_init = 0.8 - 0.6 * math.exp(-0.3 * l)
    mgf = sb(ph, "mgf", [128, 16])
    agf = sb(ph, "agf", [128, 8])
    P.dma("sync", lambda e: e.dma_start(out=mgf[:], in_=T["mg_fm"][l]), writes=["mgf"])
    P.dma("sync", lambda e: e.dma_start(out=agf[:], in_=T["ag_fm"][l]), writes=["agf"])
    P.op("vector", lambda e: e.tensor_scalar(out=agf[:, :], in0=agf[:, :], scalar1=1.0 - lam_init, scalar2=None,
                                             op0=ALU.mult), reads=["agf"], writes=["agf"])
    for ri, (src, dst, key) in enumerate(rounds):
        wb = ri % 2
        P.dma("sync", lambda e, src=src, wb=wb: e.dma_start(
            out=wst[wb][:, :, :], in_=src.rearrange("(c p) n -> p c n", p=128)), writes=[("wstD", wb)])
        if key[0] == "wo":
            P.op("vector", lambda e, dst=dst, wb=wb: e.tensor_copy(out=dst, in_=wst[wb][:, :, :]),
                 reads=[("wstD", wb)], writes=[key])
        else:
            for fc in range(8):
                sc = mgf[:, key[1] * 8 + fc:key[1] * 8 + fc + 1] if key[0] == "wpm" else agf[:, fc:fc + 1]
                eng = "scalar" if fc % 2 == 0 else "vector"
                if eng == "scalar":
                    P.op("scalar", lambda e, dst=dst, wb=wb, fc=fc, sc=sc: e.activation(
                        out=dst[:, fc, :], in_=wst[wb][:, fc, :], func=AF.Identity, scale=sc),
                        reads=[("wstD", wb), "mgf", "agf"], writes=[key + (fc,)])
                else:
                    P.op("vector", lambda e, dst=dst, wb=wb, fc=fc, sc=sc: e.tensor_scalar(
                        out=dst[:, fc, :], in0=wst[wb][:, fc, :], scalar1=sc, scalar2=None, op0=ALU.mult),
                        reads=[("wstD", wb), "mgf", "agf"], writes=[key + (fc,)])

    nJ = k.cfg.get("d2_tiles", 8)

    def load_J(J):
        cs = slice(J * 512, (J + 1) * 512)
        P.dma("sync", lambda e: e.dma_start(
            out=HMT[:, :, :], in_=T["hm_T"][:, cs].rearrange("(c p) t -> p c t", p=128)), writes=["HMT"])
        P.dma("sync", lambda e: e.dma_start(
            out=HAT[:, :, :], in_=T["ha_T"][:, cs].rearrange("(c p) t -> p c t", p=128)), writes=["HAT"])
        P.dma("sync", lambda e: e.dma_start(
            out=SGM[:, :, :], in_=T["sgm_T"][:, cs].rearrange("(c p) t -> p c t", p=128)), writes=["SGM"])
        P.dma("sync", lambda e: e.dma_start(
            out=SGA[:, :, :], in_=T["sga_T"][:, cs].rearrange("(c p) t -> p c t", p=128)), writes=["SGA"])

    def load_x(t):
        b = t % 2
        r = slice(t * 128, (t + 1) * 128)
        P.dma("sync", lambda e: e.dma_start(out=xt[b][:, :], in_=x_src[r, :]), writes=[("xtD", b)])

    load_J(0)
    for J in range(nJ):
        for oc in range(8):
            w = oc % 2
            osl = slice(oc * 128, (oc + 1) * 128)
            for fc in range(16):
                P.op("tensor", lambda e, fc=fc, osl=osl, w=w: e.matmul(
                    p_ym[w][:, :], lhsT=wpm[:, fc, osl], rhs=HMT[:, fc, :], start=(fc == 0), stop=(fc == 15)),
                    reads=[("wpm", fc // 8, oc // 4, fc % 8), "HMT"], writes=[("p_ym", w)])
            for fc in range(8):
                P.op("tensor", lambda e, fc=fc, osl=osl, w=w: e.matmul(
                    p_ya[w][:, :], lhsT=wpa[:, fc, osl], rhs=HAT[:, fc, :], start=(fc == 0), stop=(fc == 7)),
                    reads=[("wpa", oc // 4, fc), "HAT"], writes=[("p_ya", w)])
            P.op("vector", lambda e, w=w, oc=oc: e.tensor_tensor(out=t1[w][:, :], in0=p_ym[w][:, :], in1=SGM[:, oc, :],
                                                                 op=ALU.mult),
                 reads=[("p_ym", w), "SGM"], writes=[("t1", w)])
            P.op("vector", lambda e, w=w, oc=oc: e.tensor_tensor(out=t2[w][:, :], in0=p_ya[w][:, :], in1=SGA[:, oc, :],
                                                                 op=ALU.mult),
                 reads=[("p_ya", w), "SGA"], writes=[("t2", w)])
            P.op("vector", lambda e, w=w, oc=oc: e.tensor_tensor(out=YT[:, oc, :], in0=t1[w][:, :], in1=t2[w][:, :],
                                                                 op=ALU.add),
                 reads=[("t1", w), ("t2", w)], writes=[("YT", oc)])
        load_x(J * 4)
        load_x(J * 4 + 1)
        if J + 1 < nJ:
            load_J(J + 1)
        for tl in range(4):
            t = J * 4 + tl
            b = t % 2
            r = slice(t * 128, (t + 1) * 128)
            if tl >= 2:
                load_x(t)
            for half in range(2):
                pw = (tl * 2 + half) % 4
                hsl = slice(half * 512, (half + 1) * 512)
                for oc in range(8):
                    P.op("tensor", lambda e, oc=oc, tl=tl, pw=pw, hsl=hsl: e.matmul(
                        p_wo[pw][:, :], lhsT=YT[:, oc, tl * 128:(tl + 1) * 128], rhs=wo[:, oc, hsl],
                        start=(oc == 0), stop=(oc == 7)),
                        reads=[("YT", oc), ("wo", half)], writes=[("p_wo", pw)])
                P.op("vector", lambda e, b=b, pw=pw, hsl=hsl: e.tensor_tensor(
                    out=xn[b][:, hsl], in0=p_wo[pw][:, :], in1=gateb[:, l, hsl], op=ALU.mult),
                    reads=[("p_wo", pw)], writes=[("xnD", b, half)])
                P.op("vector", lambda e, b=b, hsl=hsl: e.tensor_tensor(
                    out=xn[b][:, hsl], in0=xn[b][:, hsl], in1=xt[b][:, hsl], op=ALU.add),
                    reads=[("xnD", b, half), ("xtD", b)], writes=[("xnD", b, half)])
            if not last:
                P.dma("sync", lambda e, b=b, r=r: e.dma_start(out=x_dst[r, :], in_=xn[b][:, :]),
                      reads=[("xnD", b, 0), ("xnD", b, 1)])
            else:
                P.op("scalar", lambda e, b=b: e.activation(out=junk[:, :], in_=xn[b][:, :], func=AF.Square),
                     reads=[("xnD", b, 0), ("xnD", b, 1)], writes=["junkE"])
                P.op("vector", lambda e, b=b: e.reduce_sum(out=ss[b][:, :], in_=junk[:, :], axis=AX.X),
                     reads=["junkE"], writes=[("ssE", b)])
                P.op("vector", lambda e, b=b: e.tensor_scalar(out=rs[b][:, :], in0=ss[b][:, :], scalar1=1.0 / D,
                                                              scalar2=EPS, op0=ALU.mult, op1=ALU.add),
                     reads=[("ssE", b)], writes=[("rsE", b)])
                P.op("scalar", lambda e, b=b: e.activation(out=rs[b][:, :], in_=rs[b][:, :], func=AF.Sqrt),
                     reads=[("rsE", b)], writes=[("rsE", b)])
                P.op("vector", lambda e, b=b: e.reciprocal(out=rs[b][:, :], in_=rs[b][:, :]),
                     reads=[("rsE", b)], writes=[("rsE", b)])
                P.op("vector", lambda e, b=b: e.scalar_tensor_tensor(
                    out=ot[b][:, :], in0=xn[b][:, :], scalar=rs[b][:, 0:1], in1=fg[:, :],
                    op0=ALU.mult, op1=ALU.mult),
                    reads=[("xnD", b, 0), ("xnD", b, 1), ("rsE", b), "fg"], writes=[("otD", b)])
                P.dma("sync", lambda e, b=b, r=r: e.dma_start(out=x_dst[r, :], in_=ot[b][:, :]),
                      reads=[("otD", b)])
```

```python
import numpy as np
import ml_dtypes
from contextlib import ExitStack
import concourse.bass as bass
import concourse.mybir as mybir
from concourse.bass_utils import run_bass_kernel_spmd

F32 = mybir.dt.float32
BF16 = mybir.dt.bfloat16
AF = mybir.ActivationFunctionType
ALU = mybir.AluOpType
AX = mybir.AxisListType

D = 1024
S = 4096
NT = S // 128
DEPTH = 2
N_IN = 14344
EPS = 1e-6
ENGS = ["tensor", "vector", "scalar", "gpsimd", "sync"]
NDS = 24


class _Op:
    __slots__ = ("fn", "waits", "inc", "incval", "dsem", "dinc")

    def __init__(self, fn):
        self.fn = fn
        self.waits = []
        self.inc = False
        self.incval = 0
        self.dsem = None
        self.dinc = 0


class Prog:
    def __init__(self, nc, stack):
        self.nc = nc
        self.sems = {e: stack.enter_context(nc.semaphore("s_" + e)) for e in ENGS[:4]}
        self.dsems = [stack.enter_context(nc.semaphore("d%d" % i)) for i in range(NDS)]
        self.dval = [0] * NDS
        self.dnext = 0
        self.inc_count = {e: 0 for e in ENGS[:4]}
        self.nblock = 0
        self._reset()

    def _reset(self):
        self.ops = {e: [] for e in ENGS}
        self.res = {}
        self.cw = {e: {} for e in ENGS}
        self.dw = {e: {} for e in ENGS}

    def _need(self, eng, op, tok):
        if tok[0] == "c":
            _, src, idx = tok
            if self.cw[eng].get(src, -1) >= idx:
                return
            self.cw[eng][src] = idx
            self.ops[src][idx].inc = True
            op.waits.append(tok)
        else:
            _, s, v = tok
            if self.dw[eng].get(s, 0) >= v:
                return
            self.dw[eng][s] = v
            op.waits.append(tok)

    def _rec(self, eng, op, tok, rkey, reads, writes):
        is_dma = tok[0] == "d"
        deps = []
        for r in reads:
            st = self.res.get(r)
            if st is not None and st[0] is not None:
                w = st[0]
                if not (w[0] == "c" and w[1] == eng and eng == "tensor" and not is_dma):
                    deps.append(w)
        for wk in writes:
            st = self.res.get(wk)
            if st is not None:
                w = st[0]
                if w is not None and (is_dma or w[0] == "d" or w[1] != eng):
                    deps.append(w)
                for k, t in st[1].items():
                    if is_dma or t[0] == "d" or t[1] != eng:
                        deps.append(t)
        for t in deps:
            self._need(eng, op, t)
        for r in reads:
            st = self.res.setdefault(r, [None, {}])
            st[1][rkey] = tok
        for wk in writes:
            self.res[wk] = [tok, {}]

    def op(self, eng, fn, reads=(), writes=()):
        o = _Op(fn)
        idx = len(self.ops[eng])
        tok = ("c", eng, idx)
        self._rec(eng, o, tok, eng, reads, writes)
        self.ops[eng].append(o)
        return tok

    def dma(self, eng, fn, reads=(), writes=()):
        o = _Op(fn)
        s = self.dnext
        self.dnext = (self.dnext + 1) % NDS
        if self.dval[s] > 0:
            self._need(eng, o, ("d", s, self.dval[s]))
        self.dval[s] += 16
        tok = ("d", s, self.dval[s])
        o.dsem = s
        self._rec(eng, o, tok, ("d", s), reads, writes)
        self.ops[eng].append(o)
        return tok

    def flush(self):
        nc = self.nc
        fin = _Op(None)
        for s in range(NDS):
            if self.dval[s] > 0:
                self._need("sync", fin, ("d", s, self.dval[s]))
        self.ops["sync"].append(fin)
        for e in ENGS[:4]:
            c = self.inc_count[e]
            for o in self.ops[e]:
                if o.inc:
                    c += 1
                    o.incval = c
            self.inc_count[e] = c
        ops = self.ops
        sems, dsems = self.sems, self.dsems

        def emit(e, name):
            for o in ops[name]:
                for t in o.waits:
                    if t[0] == "c":
                        e.wait_ge(sems[t[1]], ops[t[1]][t[2]].incval)
                    else:
                        e.wait_ge(dsems[t[1]], t[2])
                if o.fn is None:
                    continue
                ins = o.fn(e)
                if o.dsem is not None:
                    ins.then_inc(dsems[o.dsem], 16)
                elif o.inc:
                    ins.then_inc(sems[name], 1)

        self.nblock += 1
        with nc.Block() as block:
            @block.tensor
            def _(e):
                emit(e, "tensor")

            @block.vector
            def _(e):
                emit(e, "vector")

            @block.scalar
            def _(e):
                emit(e, "scalar")

            @block.gpsimd
            def _(e):
                emit(e, "gpsimd")

            @block.sync
            def _(e):
                emit(e, "sync")
        self._reset()


C_QM, C_KM, C_VM, C_OM, C_ZM = 0, 1024, 2048, 4096, 6144
C_IM, C_FM = 8192, 8196
C_QA, C_KA, C_VA, C_ZA, C_GM, C_GA = 8200, 9224, 10248, 11272, 12296, 13320


class K:
    pass


def build(cfg):
    nc = bass.Bass("TRN2", target_bir_lowering=False)
    k = K()
    k.nc = nc
    k.cfg = cfg
    depth = cfg.get("depth", DEPTH)
    stop = cfg.get("stop", None)
    with ExitStack() as top:
        P = Prog(nc, top)
        k.P = P

        def din(name, shape, dt=F32):
            return nc.dram_tensor(name, list(shape), dt, kind="ExternalInput").ap()

        def dscr(name, shape, dt):
            kind = "ExternalOutput" if name in cfg.get("dump", ()) else "Internal"
            return nc.dram_tensor(name, list(shape), dt, kind=kind).ap()

        uid = [0]

        def sb(stack, name, shape, dt=F32):
            uid[0] += 1
            return stack.enter_context(nc.sbuf_tensor("%s_u%d" % (name, uid[0]), list(shape), dt))

        def ps(stack, name, shape, dt=F32):
            uid[0] += 1
            return stack.enter_context(nc.psum_tensor("%s_u%d" % (name, uid[0]), list(shape), dt))

        x_in = din("x", [S, D])
        crep = din("crep", [128, 8, 128])
        w_ada = din("w_ada", [DEPTH, D, 3 * D])
        bada_fm = din("bada_fm", [DEPTH, 128, 16])
        bgate_b = din("bgate_b", [DEPTH, 128, D])
        ng_fm = din("ng_fm", [DEPTH, 128, 8])
        w_in = din("w_in", [DEPTH, D, N_IN])
        b_i = din("b_i", [DEPTH, 4, 1])
        b_f = din("b_f", [DEPTH, 4, 1])
        convw_fm = din("convw_fm", [DEPTH, 128, 16, 4])
        convb_fm = din("convb_fm", [DEPTH, 128, 16])
        mg_b = din("mg_b", [DEPTH, 128, 2048])
        mg_fm = din("mg_fm", [DEPTH, 128, 16])
        w_pm = din("w_pm", [DEPTH, 2048, D])
        w_pa = din("w_pa", [DEPTH, D, D])
        w_o = din("w_o", [DEPTH, D, D])
        lam_b = din("lam_b", [DEPTH, 128, 4, 64])
        ag_fm = din("ag_fm", [DEPTH, 128, 8])
        tbias = din("tbias", [8, 128, 640])
        fg_b = din("fg_b", [128, D])
        ident_bf_d = din("ident_bf", [128, 128], BF16)
        ident_f_d = din("ident_f", [128, 128])
        maskT_d = din("maskT", [128, 128])
        sel4_d = din("sel4", [4, 4, 128])
        out = nc.dram_tensor("out", [S, D], F32, kind="ExternalOutput").ap()

        x1 = dscr("x1", [S, D], F32)
        qkm_T = dscr("qkm_T", [2048, S], BF16)
        vm_s = dscr("vm_s", [S, 2048], BF16)
        gom_s = dscr("gom_s", [S, 2048], BF16)
        zm_s = dscr("zm_s", [S, 2048], BF16)
        qa_T = dscr("qa_T", [1024, S], BF16)
        ka_T = dscr("ka_T", [1024, S], BF16)
        va_s = dscr("va_s", [S, 1024], BF16)
        za_T = dscr("za_T", [1024, S], BF16)
        sgm_T = dscr("sgm_T", [1024, S], BF16)
        sga_T = dscr("sga_T", [1024, S], BF16)
        ha_T = dscr("ha_T", [1024, S], BF16)
        hm_T = dscr("hm_T", [2048, S], BF16)
        hT_dump = dscr("hT_dump", [1024, S], BF16) if "hT_dump" in cfg.get("dump", ()) else None

        ident_bf = sb(top, "ident_bf_s", [128, 128], BF16)
        ident_f = sb(top, "ident_f_s", [128, 128])
        modfm = sb(top, "modfm", [128, DEPTH, 16])
        gsfm = sb(top, "gsfm", [128, DEPTH, 8])
        gateb = sb(top, "gateb", [128, DEPTH, D])

        def finish(name):
            P.flush()
            return stop == name

        with ExitStack() as ph:
            ca = sb(ph, "ca", [128, 8, 128])
            wada = sb(ph, "wada", [128, 8, 3 * D])
            tmp16 = sb(ph, "tmp16", [128, 16])
            bfm = sb(ph, "bfm", [128, 16])
            ngt = sb(ph, "ngt", [128, 8])
            bgt = sb(ph, "bgt", [128, D])
            pmod = ps(ph, "pmod", [128, 512])
            pg0 = ps(ph, "pg0", [128, 512])
            pg1 = ps(ph, "pg1", [128, 512])
            P.dma("sync", lambda e: e.dma_start(out=ident_bf[:], in_=ident_bf_d), writes=["ident_bf"])
            P.dma("sync", lambda e: e.dma_start(out=ident_f[:], in_=ident_f_d), writes=["ident_f"])
            P.dma("sync", lambda e: e.dma_start(out=ca[:], in_=crep), writes=["ca"])
            P.op("scalar", lambda e: e.activation(out=ca[:], in_=ca[:], func=AF.Silu), reads=["ca"], writes=["ca"])
            scut = cfg.get("scut", 99)
            for l in range(depth if scut > 0 else 0):
                for kc in range(8):
                    P.dma("sync", lambda e, l=l, kc=kc: e.dma_start(
                        out=wada[:, kc, :], in_=w_ada[l, kc * 128:(kc + 1) * 128, :]),
                        writes=[("wada", kc)])
                P.dma("sync", lambda e, l=l: e.dma_start(out=bfm[:], in_=bada_fm[l]), writes=["bfm"])
                P.dma("sync", lambda e, l=l: e.dma_start(out=ngt[:], in_=ng_fm[l]), writes=["ngt"])
                P.dma("sync", lambda e, l=l: e.dma_start(out=bgt[:], in_=bgate_b[l]), writes=["bgt"])
                if scut < 2:
                    continue
                for f in range(16):
                    for kc in range(8):
                        P.op("tensor", lambda e, f=f, kc=kc: e.matmul(
                            pmod[:, 2 * f:2 * f + 2], lhsT=wada[:, kc, f * 128:(f + 1) * 128],
                            rhs=ca[:, kc, 0:2], start=(kc == 0), stop=(kc == 7)),
                            reads=[("wada", kc), "ca"], writes=["pmod"])
                if scut < 3:
                    continue
                for half, pg in enumerate((pg0, pg1)):
                    for kc in range(8):
                        P.op("tensor", lambda e, kc=kc, half=half, pg=pg: e.matmul(
                            pg[:, :], lhsT=ca[:, kc, :],
                            rhs=wada[:, kc, 2048 + half * 512:2048 + (half + 1) * 512],
                            start=(kc == 0), stop=(kc == 7)),
                            reads=[("wada", kc), "ca"], writes=[("pg", half)])
                if scut < 4:
                    continue
                pm_v = pmod[:, 0:32].rearrange("p (f two) -> p f two", two=2)[:, :, 0]
                P.op("vector", lambda e, l=l, pm_v=pm_v: e.tensor_tensor(
                    out=modfm[:, l, :], in0=pm_v, in1=bfm[:], op=ALU.add),
                    reads=["pmod", "bfm"], writes=[("modfm", l)])
                P.op("vector", lambda e, l=l: e.scalar_tensor_tensor(
                    out=gsfm[:, l, :], in0=modfm[:, l, 8:16], scalar=1.0, in1=ngt[:],
                    op0=ALU.add, op1=ALU.mult),
                    reads=[("modfm", l), "ngt"], writes=[("gsfm", l)])
                for half, pg in enumerate((pg0, pg1)):
                    P.op("vector", lambda e, l=l, half=half, pg=pg: e.tensor_tensor(
                        out=gateb[:, l, half * 512:(half + 1) * 512], in0=pg[:, :],
                        in1=bgt[:, half * 512:(half + 1) * 512], op=ALU.add),
                        reads=[("pg", half), "bgt"], writes=[("gateb", l, half)])
            if finish("S"):
                return nc

        for l in range(depth):
            last = (l == depth - 1)
            x_src = x_in if l == 0 else x1
            x_dst = out if last else x1
            with ExitStack() as lay:
                cols = (sb(lay, "wkcol%d" % l, [128, NT, 4]), sb(lay, "wkcolb%d" % l, [128, NT, 4], BF16),
                        sb(lay, "necol%d" % l, [128, NT, 4]), sb(lay, "dbc%d" % l, [128, 4, NT]))
                gst = ExitStack()
                lay.enter_context(gst)
                gi = sb(gst, "gi%d" % l, [4, S])
                gf = sb(gst, "gf%d" % l, [4, S])
                with ExitStack() as ab:
                    hT = sb(ab, "hT%d" % l, [128, 8, S], BF16)
                    with ExitStack() as ph:
                        xt = [sb(ph, "xt%d" % i, [128, D]) for i in range(3)]
                        junk = [sb(ph, "junkA%d" % i, [128, D]) for i in range(2)]
                        xn = [sb(ph, "xn%d" % i, [128, D], BF16) for i in range(2)]
                        ss = [sb(ph, "ssA%d" % i, [128, 1]) for i in range(3)]
                        rs = [sb(ph, "rsA%d" % i, [128, 1]) for i in range(3)]
                        ptr = [ps(ph, "ptrA%d" % i, [128, D], BF16) for i in range(2)]
                        def a_stage1(t):
                            b = t % 3
                            P.dma("sync", lambda e: e.dma_start(
                                out=xt[b][:], in_=x_src[t * 128:(t + 1) * 128, :]), writes=[("xt", b)])
                            P.op("scalar", lambda e: e.activation(out=junk[b % 2][:], in_=xt[b][:], func=AF.Square),
                                 reads=[("xt", b)], writes=[("junkA", b % 2)])
                            P.op("vector", lambda e: e.reduce_sum(out=ss[b][:], in_=junk[b % 2][:], axis=AX.X),
                                 reads=[("junkA", b % 2)], writes=[("ss", b)])
                            P.op("vector", lambda e: e.tensor_scalar(
                                out=rs[b][:], in0=ss[b][:], scalar1=1.0 / D, scalar2=EPS,
                                op0=ALU.mult, op1=ALU.add), reads=[("ss", b)], writes=[("rs", b)])

                        def a_stage2(t):
                            b = t % 3
                            pb_ = t % 2
                            P.op("scalar", lambda e: e.activation(out=rs[b][:], in_=rs[b][:], func=AF.Sqrt),
                                 reads=[("rs", b)], writes=[("rs", b)])
                            P.op("vector", lambda e: e.reciprocal(out=rs[b][:], in_=rs[b][:]),
                                 reads=[("rs", b)], writes=[("rs", b)])
                            P.op("vector", lambda e: e.tensor_scalar(
                                out=xn[pb_][:], in0=xt[b][:], scalar1=rs[b][:], scalar2=None,
                                op0=ALU.mult), reads=[("xt", b), ("rs", b)], writes=[("xn", pb_)])
                            for c in range(8):
                                P.op("tensor", lambda e, c=c: e.transpose(
                                    ptr[pb_][:, c * 128:(c + 1) * 128], xn[pb_][:, c * 128:(c + 1) * 128],
                                    ident_bf[:]), reads=[("xn", pb_), "ident_bf"], writes=[("ptr", pb_)])
                            for c in range(8):
                                if pb_ == 0:
                                    P.op("scalar", lambda e, c=c: e.activation(
                                        out=hT[:, c, t * 128:(t + 1) * 128], in_=ptr[pb_][:, c * 128:(c + 1) * 128],
                                        func=AF.Identity, bias=modfm[:, l, c:c + 1], scale=gsfm[:, l, c:c + 1]),
                                        reads=[("ptr", pb_)], writes=[("hT", c, t // 4)])
                                else:
                                    P.op("vector", lambda e, c=c: e.tensor_scalar(
                                        out=hT[:, c, t * 128:(t + 1) * 128], in0=ptr[pb_][:, c * 128:(c + 1) * 128],
                                        scalar1=gsfm[:, l, c:c + 1], scalar2=modfm[:, l, c:c + 1],
                                        op0=ALU.mult, op1=ALU.add),
                                        reads=[("ptr", pb_)], writes=[("hT", c, t // 4)])

                        a_stage1(0)
                        for t in range(NT):
                            if t + 1 < NT:
                                a_stage1(t + 1)
                            a_stage2(t)
                        if hT_dump is not None and l == cfg.get("dump_layer", 0):
                            for c in range(8):
                                P.dma("sync", lambda e, c=c: e.dma_start(
                                    out=hT_dump[c * 128:(c + 1) * 128, :], in_=hT[:, c, :]),
                                    reads=[("hT", c, j) for j in range(8)])
                        if finish("A"):
                            return nc

                    with ExitStack() as ph:
                        phase_B(k, ph, l, hT, gi, gf, dict(
                            w_in=w_in, b_i=b_i, b_f=b_f, convw_fm=convw_fm, convb_fm=convb_fm,
                            qkm_T=qkm_T, vm_s=vm_s, gom_s=gom_s, zm_s=zm_s, qa_T=qa_T, ka_T=ka_T,
                            va_s=va_s, za_T=za_T, sgm_T=sgm_T, sga_T=sga_T), sb, ps)
                        if "gates_dump" in cfg.get("dump", ()):
                            gd = nc.dram_tensor("gates_dump", [8, S], F32, kind="ExternalOutput").ap()
                            P.dma("sync", lambda e: e.dma_start(out=gd[0:4, :], in_=gi[:, :]),
                                  reads=[("gate", 0, tt) for tt in range(8)])
                            P.dma("sync", lambda e: e.dma_start(out=gd[4:8, :], in_=gf[:, :]),
                                  reads=[("gate", 1, tt) for tt in range(8)])
                        if finish("B"):
                            return nc
                Tn = dict(ident_f=ident_f, ident_bf=ident_bf, sel4_d=sel4_d, lam_b=lam_b, ag_fm=ag_fm, ka_T=ka_T,
                          qa_T=qa_T, va_s=va_s, za_T=za_T, tbias=tbias, ha_T=ha_T, maskT_d=maskT_d, mg_b=mg_b,
                          qkm_T=qkm_T, vm_s=vm_s, gom_s=gom_s, zm_s=zm_s, hm_T=hm_T, w_pm=w_pm, w_pa=w_pa,
                          w_o=w_o, sgm_T=sgm_T, sga_T=sga_T, fg_b=fg_b, mg_fm=mg_fm)
                with ExitStack() as ph:
                    phase_D0(k, ph, l, gi, gf, cols, Tn, sb, ps)
                    if finish("D0"):
                        return nc
                gst.close()
                if "C" not in cfg.get("skip", ()):
                    with ExitStack() as ph:
                        phase_C(k, ph, l, Tn, sb, ps)
                        if finish("C"):
                            return nc
                if "D1" not in cfg.get("skip", ()):
                    with ExitStack() as ph:
                        phase_D1(k, ph, l, cols, Tn, sb, ps)
                        if finish("D1"):
                            return nc
                with ExitStack() as ph:
                    phase_D2(k, ph, l, last, x_src, x_dst, gateb, Tn, sb, ps)
                    if finish("D2"):
                        return nc
        return nc


def phase_B(k, ph, l, hT, gi, gf, T, sb, ps):
    P = k.P
    w_in = T["w_in"]
    wst = [sb(ph, "wst%d" % i, [128, 8, 512]) for i in range(2)]
    wbf = [sb(ph, "wbf%d" % i, [128, 8, 512], BF16) for i in range(2)]
    ostg = [sb(ph, "ostg%d" % i, [128, S], BF16) for i in range(2)]
    xs = sb(ph, "xs", [128, S + 4])
    acc = sb(ph, "acc", [128, S])
    cw = sb(ph, "cw", [128, 16, 4])
    cb = sb(ph, "cb", [128, 16])
    bi_t = sb(ph, "bi_t", [4, 1])
    bf_t = sb(ph, "bf_t", [4, 1])
    wg = sb(ph, "wg", [128, 8, 8])
    wgb = sb(ph, "wgb", [128, 8, 8], BF16)
    pb = [ps(ph, "pB%d" % i, [128, 512]) for i in range(4)]
    P.dma("sync", lambda e: e.dma_start(out=cw[:], in_=T["convw_fm"][l]), writes=["cw"])
    P.dma("sync", lambda e: e.dma_start(out=cb[:], in_=T["convb_fm"][l]), writes=["cb"])
    P.dma("sync", lambda e: e.dma_start(out=bi_t[:], in_=T["b_i"][l]), writes=["bi_t"])
    P.dma("sync", lambda e: e.dma_start(out=bf_t[:], in_=T["b_f"][l]), writes=["bf_t"])
    P.op("gpsimd", lambda e: e.memset(xs[:, 0:4], 0.0), writes=["xs_halo"])
    hT_all = [("hT", c, j) for c in range(8) for j in range(8)]
    st = {"g": 0, "pb": 0, "os": 0}

    pending = {}

    def load_w(c0, n):
        if c0 in pending:
            return pending.pop(c0)
        g = st["g"] % 2
        st["g"] += 1
        wsrc = w_in[l, :, c0:c0 + n].rearrange("(c p) n -> p c n", p=128)
        P.dma("sync", lambda e: e.dma_start(out=wst[g][:, :, 0:n], in_=wsrc), writes=[("wst", g)])
        castq.append((g, n, 0))
        castq.append((g, n, 1))
        return g

    castq = []

    def do_cast(all_=False):
        while castq:
            g, n, hh = castq.pop(0)
            for q in range(2):
                k0 = 4 * hh + 2 * q
                P.op("scalar", lambda e, g=g, n=n, k0=k0: e.activation(
                    out=wbf[g][:, k0:k0 + 2, 0:n], in_=wst[g][:, k0:k0 + 2, 0:n], func=AF.Copy),
                    reads=[("wst", g)], writes=[("wbf", g, hh)])
            if not all_:
                break

    def next_pb():
        i = st["pb"] % 4
        st["pb"] += 1
        return i

    def conv_quarter(ch, q, o):
        lo = 4 + q * 1024
        rd = [("xs", 2 * q), ("xs", 2 * q + 1), (("xs", 2 * q - 1) if q > 0 else "xs_halo")]
        P.op("vector", lambda e: e.tensor_scalar(
            out=acc[:, q * 1024:(q + 1) * 1024], in0=xs[:, lo:lo + 1024], scalar1=cw[:, ch, 3:4],
            scalar2=cb[:, ch:ch + 1], op0=ALU.mult, op1=ALU.add), reads=rd + ["cw", "cb"], writes=[("acc", q)])
        for tap in (2, 1, 0):
            sh = 3 - tap
            P.op("vector", lambda e, tap=tap, sh=sh: e.scalar_tensor_tensor(
                out=acc[:, q * 1024:(q + 1) * 1024], in0=xs[:, lo - sh:lo - sh + 1024], scalar=cw[:, ch, tap:tap + 1],
                in1=acc[:, q * 1024:(q + 1) * 1024], op0=ALU.mult, op1=ALU.add),
                reads=rd + [("acc", q), "cw"], writes=[("acc", q)])
        siluq.append((evc[0] + 2, lambda: P.op("scalar", lambda e: e.activation(
            out=ostg[o][:, q * 1024:(q + 1) * 1024], in_=acc[:, q * 1024:(q + 1) * 1024], func=AF.Silu),
            reads=[("acc", q)], writes=[("ostg", o, 2 * q), ("ostg", o, 2 * q + 1)])))

    siluq = []
    evc = [0]
    convst = []

    def pop_silu(all_=False):
        while siluq and (all_ or siluq[0][0] <= evc[0]):
            siluq.pop(0)[1]()

    def fm_group(c0, ncols, kind, dst, drow0, convch0=None):
        g = load_w(c0, ncols)
        for j in range(ncols // 128):
            o = st["os"] % 2
            st["os"] += 1
            for tt in range(8):
                pi = next_pb()
                for kc in range(8):
                    P.op("tensor", lambda e, pi=pi, kc=kc, j=j, tt=tt: e.matmul(
                        pb[pi][:, :], lhsT=wbf[g][:, kc, j * 128:(j + 1) * 128],
                        rhs=hT[:, kc, tt * 512:(tt + 1) * 512], start=(kc == 0), stop=(kc == 7)),
                        reads=[("wbf", g, kc // 4), ("hT", kc, tt)], writes=[("pb", pi)])
                sl = slice(tt * 512, (tt + 1) * 512)
                if kind == "conv":
                    P.op("scalar", lambda e, pi=pi, tt=tt: e.activation(
                        out=xs[:, 4 + tt * 512:4 + (tt + 1) * 512], in_=pb[pi][:, :], func=AF.Copy),
                        reads=[("pb", pi)], writes=[("xs", tt)])
                    evc[0] += 1
                    pop_silu()
                    if tt % 2 == 1:
                        conv_quarter(convch0 + j, tt // 2, o)
                elif kind == "copy":
                    if tt % 2 == 0:
                        P.op("scalar", lambda e, pi=pi, sl=sl, o=o: e.activation(
                            out=ostg[o][:, sl], in_=pb[pi][:, :], func=AF.Copy),
                            reads=[("pb", pi)], writes=[("ostg", o, tt)])
                    else:
                        P.op("vector", lambda e, pi=pi, sl=sl, o=o: e.tensor_copy(
                            out=ostg[o][:, sl], in_=pb[pi][:, :]),
                            reads=[("pb", pi)], writes=[("ostg", o, tt)])
                else:
                    fn = AF.Silu if kind == "silu" else AF.Sigmoid
                    P.op("scalar", lambda e, pi=pi, sl=sl, o=o, fn=fn: e.activation(
                        out=ostg[o][:, sl], in_=pb[pi][:, :], func=fn),
                        reads=[("pb", pi)], writes=[("ostg", o, tt)])
            r0 = drow0 + j * 128
            if kind == "conv":
                if convst:
                    pop_prev = convst.pop(0)
                    pop_prev()
                convst.append(lambda o=o, r0=r0: P.dma("sync", lambda e: e.dma_start(
                    out=dst[r0:r0 + 128, :], in_=ostg[o][:, :]), reads=[("ostg", o, tt) for tt in range(8)]))
            else:
                P.dma("sync", lambda e, o=o, r0=r0: e.dma_start(out=dst[r0:r0 + 128, :], in_=ostg[o][:, :]),
                      reads=[("ostg", o, tt) for tt in range(8)])
            if j in (1, 2):
                do_cast()

    def tm_group(c0, kind, dst, dcol0):
        g = load_w(c0, 512)
        for q4 in range(4):
            o = st["os"] % 2
            st["os"] += 1
            for t8 in range(8):
                t = q4 * 8 + t8
                pi = next_pb()
                for kc in range(8):
                    P.op("tensor", lambda e, pi=pi, kc=kc, t=t: e.matmul(
                        pb[pi][:, :], lhsT=hT[:, kc, t * 128:(t + 1) * 128], rhs=wbf[g][:, kc, :],
                        start=(kc == 0), stop=(kc == 7)),
                        reads=[("wbf", g, kc // 4), ("hT", kc, t // 4)], writes=[("pb", pi)])
                sl = slice(t8 * 512, (t8 + 1) * 512)
                if kind == "copy":
                    if t8 % 2 == 0:
                        P.op("scalar", lambda e, pi=pi, sl=sl, o=o: e.activation(
                            out=ostg[o][:, sl], in_=pb[pi][:, :], func=AF.Copy),
                            reads=[("pb", pi)], writes=[("ostg", o, t8)])
                    else:
                        P.op("vector", lambda e, pi=pi, sl=sl, o=o: e.tensor_copy(
                            out=ostg[o][:, sl], in_=pb[pi][:, :]),
                            reads=[("pb", pi)], writes=[("ostg", o, t8)])
                else:
                    fn = AF.Silu if kind == "silu" else AF.Sigmoid
                    P.op("scalar", lambda e, pi=pi, sl=sl, o=o, fn=fn: e.activation(
                        out=ostg[o][:, sl], in_=pb[pi][:, :], func=fn),
                        reads=[("pb", pi)], writes=[("ostg", o, t8)])
            dv = dst[q4 * 1024:(q4 + 1) * 1024, dcol0:dcol0 + 512].rearrange("(t p) c -> p t c", p=128)
            P.dma("sync", lambda e, o=o, dv=dv: e.dma_start(
                out=dv, in_=ostg[o][:, :].rearrange("p (t c) -> p t c", c=512)),
                reads=[("ostg", o, t8) for t8 in range(8)])
            if q4 in (1, 2):
                do_cast()

    only = k.cfg.get("b_only", None)

    def want(name):
        return only is None or name in only

    if want("gates"):
        wsrc = w_in[l, :, C_IM:C_IM + 8].rearrange("(c p) n -> p c n", p=128)
        P.dma("sync", lambda e: e.dma_start(out=wg[:], in_=wsrc), writes=["wg"])
        P.op("gpsimd", lambda e: e.tensor_copy(out=wgb[:], in_=wg[:]), reads=["wg"], writes=["wgb"])
        for which, (gt, bt) in enumerate(((gi, bi_t), (gf, bf_t))):
            for tt in range(8):
                pi = next_pb()
                for kc in range(8):
                    P.op("tensor", lambda e, pi=pi, kc=kc, tt=tt, which=which: e.matmul(
                        pb[pi][0:4, :], lhsT=wgb[:, kc, 4 * which:4 * which + 4],
                        rhs=hT[:, kc, tt * 512:(tt + 1) * 512], start=(kc == 0), stop=(kc == 7)),
                        reads=["wgb", ("hT", kc, tt)], writes=[("pb", pi)])
                P.op("scalar", lambda e, pi=pi, tt=tt, gt=gt, bt=bt: e.activation(
                    out=gt[:, tt * 512:(tt + 1) * 512], in_=pb[pi][0:4, :], func=AF.Identity,
                    bias=bt[:, 0:1], scale=1.0),
                    reads=[("pb", pi), "bi_t", "bf_t"], writes=[("gate", which, tt)])
    groups = []
    if want("qkm"):
        for gidx in range(4):
            groups.append((C_QM + gidx * 512, lambda gidx=gidx: fm_group(
                C_QM + gidx * 512, 512, "conv", T["qkm_T"], gidx * 512, convch0=gidx * 4)))
    if want("vm"):
        for gidx in range(4):
            groups.append((C_VM + gidx * 512, lambda gidx=gidx: tm_group(
                C_VM + gidx * 512, "copy", T["vm_s"], gidx * 512)))
    if want("om"):
        for gidx in range(4):
            groups.append((C_OM + gidx * 512, lambda gidx=gidx: tm_group(
                C_OM + gidx * 512, "sigmoid", T["gom_s"], gidx * 512)))
    if want("zm"):
        for gidx in range(4):
            groups.append((C_ZM + gidx * 512, lambda gidx=gidx: tm_group(
                C_ZM + gidx * 512, "silu", T["zm_s"], gidx * 512)))
    if want("qa"):
        for gidx in range(2):
            groups.append((C_QA + gidx * 512, lambda gidx=gidx: fm_group(
                C_QA + gidx * 512, 512, "copy", T["qa_T"], gidx * 512)))
    if want("ka"):
        for gidx in range(2):
            groups.append((C_KA + gidx * 512, lambda gidx=gidx: fm_group(
                C_KA + gidx * 512, 512, "copy", T["ka_T"], gidx * 512)))
    if want("va"):
        for gidx in range(2):
            groups.append((C_VA + gidx * 512, lambda gidx=gidx: tm_group(
                C_VA + gidx * 512, "copy", T["va_s"], gidx * 512)))
    if want("za"):
        for gidx in range(2):
            groups.append((C_ZA + gidx * 512, lambda gidx=gidx: fm_group(
                C_ZA + gidx * 512, 512, "silu", T["za_T"], gidx * 512)))
    if want("gm"):
        for gidx in range(2):
            groups.append((C_GM + gidx * 512, lambda gidx=gidx: fm_group(
                C_GM + gidx * 512, 512, "sigmoid", T["sgm_T"], gidx * 512)))
    if want("ga"):
        for gidx in range(2):
            groups.append((C_GA + gidx * 512, lambda gidx=gidx: fm_group(
                C_GA + gidx * 512, 512, "sigmoid", T["sga_T"], gidx * 512)))

    if groups:
        pending[groups[0][0]] = load_w(groups[0][0], 512)
        do_cast(all_=True)
    for i, (c0, fn) in enumerate(groups):
        if i + 1 < len(groups):
            nxt = groups[i + 1][0]
            g_ = load_w(nxt, 512)
            pending[nxt] = g_
        fn()
        if i == 3 or i + 1 == len(groups):
            pop_silu(all_=True)
            while convst:
                convst.pop(0)()


def _t5_bucket_np(n):
    n = np.asarray(n)
    nf = np.maximum(n, 1).astype(np.float32)
    large = 16 + (np.log(nf / np.float32(16)) / np.float32(np.log(128 / 16)) * np.float32(16)).astype(np.int32)
    large = np.minimum(large, 31)
    return np.where(n < 16, n, large)


def prep_inputs(inp):
    f32 = np.float32
    A = {k_: np.asarray(v) for k_, v in inp.items()}
    shared = {}
    shared["w_ada"] = np.ascontiguousarray(A["w_ada"], dtype=f32)
    b_ada = A["b_ada"].astype(f32)
    shared["bada_fm"] = np.ascontiguousarray(b_ada[:, :2048].reshape(DEPTH, 16, 128).transpose(0, 2, 1))
    shared["bgate_b"] = np.ascontiguousarray(np.broadcast_to(b_ada[:, None, 2048:], (DEPTH, 128, D)))
    shared["ng_fm"] = np.ascontiguousarray(A["norm_g"].astype(f32).reshape(DEPTH, 8, 128).transpose(0, 2, 1))
    shared["w_in"] = np.ascontiguousarray(A["w_in"], dtype=f32)
    shared["b_i"] = np.ascontiguousarray(A["b_if"][:, :4].astype(f32).reshape(DEPTH, 4, 1))
    shared["b_f"] = np.ascontiguousarray(A["b_if"][:, 4:].astype(f32).reshape(DEPTH, 4, 1))
    shared["convw_fm"] = np.ascontiguousarray(
        A["conv_w"].astype(f32).reshape(DEPTH, 4, 16, 128).transpose(0, 3, 2, 1))
    shared["convb_fm"] = np.ascontiguousarray(A["conv_b"].astype(f32).reshape(DEPTH, 16, 128).transpose(0, 2, 1))
    shared["mg_b"] = np.ascontiguousarray(np.broadcast_to(A["mnorm_g"].astype(f32)[:, None, :], (DEPTH, 128, 2048)))
    shared["mg_fm"] = np.ascontiguousarray(A["mnorm_g"].astype(f32).reshape(DEPTH, 16, 128).transpose(0, 2, 1))
    shared["w_pm"] = np.ascontiguousarray(A["w_proj_m"], dtype=f32)
    shared["w_pa"] = np.ascontiguousarray(A["w_proj_a"], dtype=f32)
    shared["w_o"] = np.ascontiguousarray(A["w_out"], dtype=f32)
    lam4 = np.stack([A["lam_q1"], A["lam_k1"], A["lam_q2"], A["lam_k2"]], axis=1).astype(f32)
    shared["lam_b"] = np.ascontiguousarray(np.broadcast_to(lam4[:, None], (DEPTH, 128, 4, 64)))
    shared["ag_fm"] = np.ascontiguousarray(A["anorm_g"].astype(f32).reshape(DEPTH, 8, 128).transpose(0, 2, 1))
    dlt = np.arange(640)[None, :] - np.arange(128)[:, None]
    bk = _t5_bucket_np(np.maximum(dlt, 0))
    tb = A["rel_bias"].astype(f32)[bk]
    tb = np.where((dlt >= 0)[..., None], tb, f32(-30000.0)).transpose(2, 0, 1)
    shared["tbias"] = np.ascontiguousarray(tb, dtype=f32)
    shared["fg_b"] = np.ascontiguousarray(np.broadcast_to(A["final_g"].astype(f32)[None, :], (128, D)))
    shared["ident_bf"] = np.eye(128, dtype=f32).astype(ml_dtypes.bfloat16)
    shared["ident_f"] = np.eye(128, dtype=f32)
    shared["maskT"] = np.triu(np.ones((128, 128), dtype=f32))
    sel = np.zeros((4, 4, 128), dtype=f32)
    for h in range(4):
        sel[h, h, :] = 1.0
    shared["sel4"] = sel
    maps = []
    x = A["x"].astype(f32)
    c = A["c"].astype(f32)
    for b in range(8):
        m = dict(shared)
        m["x"] = np.ascontiguousarray(x[b])
        m["crep"] = np.ascontiguousarray(np.broadcast_to(c[b].reshape(8, 128).T[:, :, None], (128, 8, 128)))
        maps.append(m)
    return maps


def kernel(**inputs):
    maps = prep_inputs(inputs)
    nc = build({})
    res = run_bass_kernel_spmd(nc, maps, core_ids=list(range(8)))
    return np.stack([np.asarray(r["out"]) for r in res.results], axis=0).astype(np.float32)


def phase_D0(k, ph, l, gi, gf, cols, T, sb, ps):
    P = k.P
    wkcol, wkcol_bf, necol, dbc = cols
    ident_f = T["ident_f"]
    ones4 = sb(ph, "ones4", [4, S])
    e1 = sb(ph, "e1", [4, S])
    Bn = sb(ph, "Bn", [4, S])
    U = sb(ph, "U", [4, S])
    A = sb(ph, "A", [4, S])
    W = sb(ph, "Wr", [4, S])
    N = sb(ph, "Nr", [4, S])
    ash = sb(ph, "ash", [4, 33])
    drow = sb(ph, "drow", [4, 32])
    sel = sb(ph, "sel", [4, 4, 128])
    pcw = ps(ph, "pcw", [128, 512])
    pcn = ps(ph, "pcn", [128, 512])
    pdb = ps(ph, "pdb", [128, 512])
    P.op("gpsimd", lambda e: e.memset(ones4[:, :], 1.0), writes=["ones4"])
    P.dma("sync", lambda e: e.dma_start(out=sel[:], in_=T["sel4_d"]), writes=["sel"])
    P.op("scalar", lambda e: e.activation(out=e1[:, :], in_=gf[:, :], func=AF.Exp, scale=-1.0), writes=["e1"])
    P.op("scalar", lambda e: e.activation(out=e1[:, :], in_=e1[:, :], func=AF.Ln, bias=1.0),
         reads=["e1"], writes=["e1"])
    P.op("vector", lambda e: e.tensor_tensor_scan(out=Bn[:, :], data0=ones4[:, :], data1=e1[:, :],
                                                  initial=0.0, op0=ALU.mult, op1=ALU.add),
         reads=["ones4", "e1"], writes=["Bn"])
    P.op("vector", lambda e: e.tensor_tensor(out=U[:, :], in0=gi[:, :], in1=Bn[:, :], op=ALU.add),
         reads=["Bn"], writes=["U"])
    P.op("vector", lambda e: e.tensor_tensor_scan(out=A[:, :], data0=ones4[:, :], data1=U[:, :],
                                                  initial=0.0, op0=ALU.mult, op1=ALU.max),
         reads=["ones4", "U"], writes=["A"])
    A3 = A[:, :].rearrange("p (c t) -> p c t", t=128)
    alb = A3[:, :, 127:128].to_broadcast([4, 32, 128])
    P.op("vector", lambda e: e.tensor_tensor(
        out=W[:, :].rearrange("p (c t) -> p c t", t=128), in0=U[:, :].rearrange("p (c t) -> p c t", t=128),
        in1=alb, op=ALU.subtract), reads=["U", "A"], writes=["W"])
    P.op("vector", lambda e: e.tensor_tensor(
        out=N[:, :].rearrange("p (c t) -> p c t", t=128), in0=Bn[:, :].rearrange("p (c t) -> p c t", t=128),
        in1=alb, op=ALU.subtract), reads=["Bn", "A"], writes=["N"])
    P.op("scalar", lambda e: e.activation(out=W[:, :], in_=W[:, :], func=AF.Exp), reads=["W"], writes=["W"])
    P.op("scalar", lambda e: e.activation(out=N[:, :], in_=N[:, :], func=AF.Exp), reads=["N"], writes=["N"])
    P.op("vector", lambda e: e.memset(ash[:, 0:1], 0.0), writes=["ash0"])
    P.op("vector", lambda e: e.tensor_copy(out=ash[:, 1:33], in_=A3[:, :, 127]), reads=["A"], writes=["ash1"])
    P.op("vector", lambda e: e.tensor_tensor(out=drow[:, :], in0=ash[:, 0:32], in1=ash[:, 1:33], op=ALU.subtract),
         reads=["ash0", "ash1"], writes=["drow"])
    P.op("scalar", lambda e: e.activation(out=drow[:, :], in_=drow[:, :], func=AF.Exp),
         reads=["drow"], writes=["drow"])
    for c in range(NT):
        P.op("tensor", lambda e, c=c: e.transpose(pcw[:, 4 * c:4 * c + 4], W[0:4, c * 128:(c + 1) * 128],
                                                  ident_f[0:4, 0:4]), reads=["W"], writes=["pcw"])
    for c in range(NT):
        P.op("tensor", lambda e, c=c: e.transpose(pcn[:, 4 * c:4 * c + 4], N[0:4, c * 128:(c + 1) * 128],
                                                  ident_f[0:4, 0:4]), reads=["N"], writes=["pcn"])
    P.op("vector", lambda e: e.tensor_scalar(out=wkcol[:, :, :].rearrange("p c h -> p (c h)"), in0=pcw[:, 0:128],
                                             scalar1=0.0625, scalar2=None, op0=ALU.mult),
         reads=["pcw"], writes=["wkcol"])
    P.op("vector", lambda e: e.tensor_copy(out=wkcol_bf[:, :, :].rearrange("p c h -> p (c h)"), in_=pcw[:, 0:128]),
         reads=["pcw"], writes=["wkcol_bf"])
    P.op("vector", lambda e: e.tensor_copy(out=necol[:, :, :].rearrange("p c h -> p (c h)"), in_=pcn[:, 0:128]),
         reads=["pcn"], writes=["necol"])
    for h in range(4):
        P.op("tensor", lambda e, h=h: e.matmul(pdb[:, h * 32:(h + 1) * 32], lhsT=sel[:, h, :], rhs=drow[:, :],
                                               start=True, stop=True), reads=["sel", "drow"], writes=["pdb"])
    P.op("vector", lambda e: e.tensor_copy(out=dbc[:, :, :].rearrange("p h c -> p (h c)"), in_=pdb[:, 0:128]),
         reads=["pdb"], writes=["dbc"])


def phase_C(k, ph, l, T, sb, ps):
    import math
    P = k.P
    lam_init = 0.8 - 0.6 * math.exp(-0.3 * l)
    ones_bf = sb(ph, "onesbf", [128, 128], BF16)
    ones_f = sb(ph, "onesf", [128, 128])
    lamt = sb(ph, "lamt", [128, 4, 64])
    lprod = sb(ph, "lprod", [128, 2, 64])
    lsum = sb(ph, "lsum", [128, 2])
    neglam = sb(ph, "neglam", [128, 1])
    epsC = sb(ph, "epsC", [128, 1])
    KT = [sb(ph, "KT%d" % i, [128, 2, S], BF16) for i in range(2)]
    TB8 = [sb(ph, "TB8_%d" % i, [128, 640], BF16) for i in range(2)]
    ident_bf = T["ident_bf"]
    QT = [sb(ph, "QT%d" % i, [128, S], BF16) for i in range(2)]
    ZA = [sb(ph, "ZA%d" % i, [128, S], BF16) for i in range(2)]
    V = [sb(ph, "V%d" % i, [128, NT, 128], BF16) for i in range(2)]
    TB = [sb(ph, "TB%d" % i, [128, 640]) for i in range(2)]
    cst = [sb(ph, "cst%d" % i, [128, 1]) for i in range(2)]
    HA = [sb(ph, "HA%d" % i, [128, S], BF16) for i in range(2)]
    PT = [sb(ph, "PT%d" % i, [128, 2, 512], BF16) for i in range(2)]
    stmp = [sb(ph, "stmp%d" % i, [128, 2, 512]) for i in range(2)]
    oc = [[sb(ph, "oc%d_%d" % (i, m), [128, 512]) for m in range(2)] for i in range(3)]
    mc = [[sb(ph, "mc%d_%d" % (i, m), [128, 512]) for m in range(2)] for i in range(3)]
    o_t = sb(ph, "o_t", [128, 512])
    sq = sb(ph, "sq_t", [128, 512])
    rstd = sb(ph, "rstdC", [128, 512])
    pS = [ps(ph, "pS%d" % i, [128, 1024]) for i in range(2)]
    pO = [ps(ph, "pO%d" % i, [128, 512]) for i in range(2)]
    pM = [ps(ph, "pM%d" % i, [128, 512]) for i in range(2)]

    P.op("gpsimd", lambda e: e.memset(ones_bf[:, :], 1.0), writes=["ones_bf"])
    P.op("gpsimd", lambda e: e.memset(ones_f[:, :], 1.0), writes=["ones_f"])
    P.op("gpsimd", lambda e: e.memset(epsC[:, :], EPS), writes=["epsC"])
    for b_ in range(2):
        P.op("vector", lambda e, b_=b_: e.memset(KT[b_][:, :, :].rearrange("p m s -> p (m s)"), 0.0),
             writes=[("KT", b_)])
    P.dma("sync", lambda e: e.dma_start(out=lamt[:], in_=T["lam_b"][l]), writes=["lamt"])
    P.op("vector", lambda e: e.tensor_tensor(out=lprod[:, :, :], in0=lamt[:, 0:4:2, :], in1=lamt[:, 1:4:2, :],
                                             op=ALU.mult), reads=["lamt"], writes=["lprod"])
    P.op("vector", lambda e: e.reduce_sum(out=lsum[:, :], in_=lprod[:, :, :], axis=AX.X),
         reads=["lprod"], writes=["lsum"])
    P.op("scalar", lambda e: e.activation(out=lsum[:, :], in_=lsum[:, :], func=AF.Exp),
         reads=["lsum"], writes=["lsum"])
    P.op("vector", lambda e: e.tensor_tensor(out=neglam[:, :], in0=lsum[:, 1:2], in1=lsum[:, 0:1],
                                             op=ALU.subtract), reads=["lsum"], writes=["neglam"])
    P.op("vector", lambda e: e.tensor_scalar(out=neglam[:, :], in0=neglam[:, :], scalar1=-lam_init, scalar2=None,
                                             op0=ALU.add), reads=["neglam"], writes=["neglam"])

    heads = k.cfg.get("c_heads", list(range(8)))
    jmax = k.cfg.get("c_jmax", 8)

    def load_head(i, h):
        b = i % 2
        r = slice(h * 128, (h + 1) * 128)
        for m in range(2):
            P.dma("sync", lambda e, m=m: e.dma_start(
                out=KT[b][m * 64:(m + 1) * 64, m, :], in_=T["ka_T"][h * 128 + m * 64:h * 128 + (m + 1) * 64, :]),
                writes=[("KT", b)])
        P.dma("sync", lambda e: e.dma_start(out=QT[b][:, :], in_=T["qa_T"][r, :]), writes=[("QT", b)])
        P.dma("sync", lambda e: e.dma_start(out=TB[b][:, :], in_=T["tbias"][h]), writes=[("TB", b)])
        P.dma("sync", lambda e: e.dma_start(
            out=V[b][:, :, :], in_=T["va_s"][:, r].rearrange("(t p) c -> p t c", p=128)), writes=[("V", b)])
        P.dma("sync", lambda e: e.dma_start(out=ZA[b][:, :], in_=T["za_T"][r, :]), writes=[("ZA", b)])
        P.op("vector", lambda e: e.tensor_copy(out=cst[b][:, :], in_=TB[b][:, 639:640]),
             reads=[("TB", b)], writes=[("cst", b)])
        P.op("vector", lambda e: e.tensor_scalar(out=TB8[b][:, :], in0=TB[b][:, :], scalar1=cst[b][:, 0:1],
                                                 scalar2=8.0, op0=ALU.subtract, op1=ALU.mult),
             reads=[("TB", b), ("cst", b)], writes=[("TB8", b)])

    pairs = []
    for i, h in enumerate(heads):
        for j in range(jmax):
            nkb = 4 * j + 4
            for kb in range(nkb):
                pairs.append(dict(i=i, h=h, j=j, kb=kb, first=(kb == 0), last=(kb == nkb - 1)))
    n = len(pairs)
    ring = [0]
    deferred = []
    tilecnt = [0]

    def geom(s):
        rel = s["kb"] - 4 * s["j"]
        return rel, max(0, rel) * 128

    def rec_qk(pi):
        s = pairs[pi]
        b = s["i"] % 2
        rel, c0 = geom(s)
        buf = pi % 2
        s["buf"] = buf
        j, kb = s["j"], s["kb"]
        if rel >= -1:
            x0 = 128 if rel == -1 else 0
            nn = 512 - c0
            for m in range(2):
                P.op("tensor", lambda e, m=m: e.matmul(
                    pS[buf][:, m * 512 + c0:(m + 1) * 512], lhsT=KT[b][:, m, kb * 128:(kb + 1) * 128],
                    rhs=QT[b][:, j * 512 + c0:(j + 1) * 512], start=True, stop=False),
                    reads=[("KT", b), ("QT", b)], writes=[("pS", buf)])
                nb = 128 if rel == -1 else min(nn, 256)
                P.op("tensor", lambda e, m=m, nb=nb: e.matmul(
                    pS[buf][:, m * 512 + c0:m * 512 + c0 + nb], lhsT=ident_bf[:, :], rhs=TB8[b][:, x0:x0 + nb],
                    start=False, stop=True),
                    reads=[("TB8", b)], writes=[("pS", buf)])
        else:
            for m in range(2):
                P.op("tensor", lambda e, m=m: e.matmul(
                    pS[buf][:, m * 512 + c0:(m + 1) * 512],
                    lhsT=KT[b][m * 64:(m + 1) * 64, m, kb * 128:(kb + 1) * 128],
                    rhs=QT[b][m * 64:(m + 1) * 64, j * 512 + c0:(j + 1) * 512], start=True, stop=True),
                    reads=[("KT", b), ("QT", b)], writes=[("pS", buf)])

    def rec_exp(pi):
        s = pairs[pi]
        b = s["i"] % 2
        rel, c0 = geom(s)
        buf = s["buf"]
        pt = pi % 2
        nn = 512 - c0
        psv = pS[buf][:, :].rearrange("p (m q) -> p m q", m=2)[:, :, c0:512]
        P.op("scalar", lambda e: e.activation(out=PT[pt][:, :, c0:512], in_=psv, func=AF.Exp, scale=0.125),
             reads=[("pS", buf)], writes=[("PT", pt)])

    def rec_av(pi):
        s = pairs[pi]
        b = s["i"] % 2
        rel, c0 = geom(s)
        pt = pi % 2
        kb = s["kb"]
        for m in range(2):
            P.op("tensor", lambda e, m=m: e.matmul(pO[m][:, c0:512], lhsT=V[b][:, kb, :], rhs=PT[pt][:, m, c0:512],
                                                   start=s["first"], stop=s["last"]),
                 reads=[("V", b), ("PT", pt)], writes=[("pO", m)])
        for m in range(2):
            P.op("tensor", lambda e, m=m: e.matmul(pM[m][:, c0:512], lhsT=ones_bf[:, :], rhs=PT[pt][:, m, c0:512],
                                                   start=s["first"], stop=s["last"]),
                 reads=["ones_bf", ("PT", pt)], writes=[("pM", m)])

    def epi1(s):
        tb = tilecnt[0] % 3
        tilecnt[0] += 1
        s["tb"] = tb
        for m in range(2):
            P.op("vector", lambda e, m=m: e.tensor_copy(out=oc[tb][m][:, :], in_=pO[m][:, :]),
                 reads=[("pO", m)], writes=[("oc", tb, m)])
            P.op("vector", lambda e, m=m: e.tensor_copy(out=mc[tb][m][:, :], in_=pM[m][:, :]),
                 reads=[("pM", m)], writes=[("mc", tb, m, q) for q in range(4)])

    def epi_pieces(s):
        b = s["i"] % 2
        h, j, tb = s["h"], s["j"], s["tb"]
        pcs = []
        for m in range(2):
            for q in range(4):
                qs = slice(q * 128, (q + 1) * 128)
                pcs.append(lambda buf, m=m, q=q, qs=qs: P.op(
                    "vector", lambda e: e.reciprocal(out=mc[tb][m][:, qs], in_=mc[tb][m][:, qs]),
                    reads=[("mc", tb, m, q)], writes=[("mc", tb, m, q)]))
        pcs.append(lambda buf: P.op("vector", lambda e: e.tensor_tensor(
            out=oc[tb][0][:, :], in0=oc[tb][0][:, :], in1=mc[tb][0][:, :], op=ALU.mult),
            reads=[("oc", tb, 0)] + [("mc", tb, 0, q) for q in range(4)], writes=[("oc", tb, 0)]))

        pcs.append(lambda buf: P.op("vector", lambda e: e.scalar_tensor_tensor(
            out=oc[tb][1][:, :], in0=oc[tb][1][:, :], scalar=neglam[:, 0:1], in1=mc[tb][1][:, :],
            op0=ALU.mult, op1=ALU.mult),
            reads=[("oc", tb, 1), "neglam"] + [("mc", tb, 1, q) for q in range(4)], writes=[("oc", tb, 1)]))

        def pc_o(buf):
            P.op("gpsimd", lambda e: e.tensor_tensor(out=o_t[:, :], in0=oc[tb][0][:, :], in1=oc[tb][1][:, :],
                                                     op=ALU.add),
                 reads=[("oc", tb, 0), ("oc", tb, 1)], writes=["o_t"])
            P.op("gpsimd", lambda e: e.tensor_tensor(out=sq[:, :], in0=o_t[:, :], in1=o_t[:, :], op=ALU.mult),
                 reads=["o_t"], writes=["sq"])
        pcs.append(pc_o)
        pcs.append(lambda buf: None)

        def pc_ssq(buf):
            P.op("tensor", lambda e: e.matmul(pS[buf][:, 0:512], lhsT=ones_f[:, :], rhs=sq[:, :],
                                              start=True, stop=True),
                 reads=["ones_f", "sq"], writes=[("pS", buf)])
            P.op("vector", lambda e: e.tensor_scalar(out=rstd[:, :], in0=pS[buf][:, 0:512], scalar1=1.0 / 128,
                                                     scalar2=EPS, op0=ALU.mult, op1=ALU.add),
                 reads=[("pS", buf)], writes=["rstd"])
        pcs.append(pc_ssq)
        pcs.append(lambda buf: None)

        def pc_rs(buf):
            P.op("scalar", lambda e: e.activation(out=rstd[:, :], in_=rstd[:, :], func=AF.Ln),
                 reads=["rstd"], writes=["rstd"])
            P.op("scalar", lambda e: e.activation(out=rstd[:, :], in_=rstd[:, :], func=AF.Exp, scale=-0.5),
                 reads=["rstd"], writes=["rstd"])
        pcs.append(pc_rs)
        pcs.append(lambda buf: None)
        pcs.append(lambda buf: P.op("vector", lambda e: e.tensor_tensor(
            out=rstd[:, :], in0=rstd[:, :], in1=o_t[:, :], op=ALU.mult),
            reads=["rstd", "o_t"], writes=["rstd"]))

        def pc_fin(buf):
            P.op("vector", lambda e: e.tensor_tensor(
                out=HA[b][:, j * 512:(j + 1) * 512], in0=rstd[:, :], in1=ZA[b][:, j * 512:(j + 1) * 512],
                op=ALU.mult), reads=["rstd", ("ZA", b)], writes=[("HA", b, j)])
            if j == jmax - 1:
                P.dma("sync", lambda e: e.dma_start(out=T["ha_T"][h * 128:(h + 1) * 128, 0:jmax * 512],
                                                    in_=HA[b][:, 0:jmax * 512]),
                      reads=[("HA", b, jj) for jj in range(jmax)])
        pcs.append(pc_fin)
        return pcs

    fifo = []
    load_head(0, heads[0])
    rec_qk(0)
    for pi in range(n):
        s = pairs[pi]
        if s["kb"] == 0 and s["j"] == 0 and s["i"] + 1 < len(heads):
            while fifo:
                fifo.pop(0)(1 - s["buf"])
            load_head(s["i"] + 1, heads[s["i"] + 1])
        if pi + 1 < n:
            rec_qk(pi + 1)
        rec_exp(pi)
        rec_av(pi)
        if s["last"]:
            epi1(s)
            fifo.extend(epi_pieces(s))
        for _ in range(2 if len(fifo) > 10 else 1):
            if fifo:
                fifo.pop(0)(s["buf"])
    while fifo:
        fifo.pop(0)(pairs[n - 1]["buf"])


def phase_D1(k, ph, l, cols, T, sb, ps):
    P = k.P
    wkcol, wkcol_bf, necol, dbc = cols
    ident_bf = T["ident_bf"]
    maskT = sb(ph, "maskT", [128, 128])
    C32 = sb(ph, "C32", [128, 4, 2, 512])
    Cbf = sb(ph, "Cbf", [128, 4, 2, 512], BF16)
    n32 = sb(ph, "n32", [128, 4, 2])
    nbf = sb(ph, "nbf", [128, 4, 2], BF16)
    qk = [sb(ph, "qk%d" % i, [128, 16, 512], BF16) for i in range(2)]
    vt = [sb(ph, "vt%d" % i, [128, 2048], BF16) for i in range(2)]
    gt = [sb(ph, "gt%d" % i, [128, 2048], BF16) for i in range(2)]
    zt = [sb(ph, "zt%d" % i, [128, 2048], BF16) for i in range(2)]
    hmT = [sb(ph, "hmT%d" % i, [128, 16, 512], BF16) for i in range(2)]
    sTm = [sb(ph, "sTm%d" % i, [128, 128], BF16) for i in range(2)]
    ktm = [sb(ph, "ktm%d" % i, [128, 256], BF16) for i in range(2)]
    hg = [sb(ph, "hg%d" % i, [128, 512]) for i in range(2)]
    hm = [sb(ph, "hm%d" % i, [128, 512], BF16) for i in range(2)]
    junk = sb(ph, "junkD", [128, 512])
    rden = [sb(ph, "rden%d" % i, [128, 1]) for i in range(2)]
    ssq = [sb(ph, "ssqD%d" % i, [128, 1]) for i in range(2)]
    rsd = [sb(ph, "rsdD%d" % i, [128, 1]) for i in range(2)]
    p_sT = ps(ph, "p_sT", [128, 512])
    p_kt = ps(ph, "p_kt", [128, 1024], BF16)
    p_num = ps(ph, "p_num", [128, 512])
    p_den = ps(ph, "p_den", [128, 512])
    p_pn = ps(ph, "p_pn", [128, 512])
    p_C = ps(ph, "p_C", [128, 1024])
    onesc = sb(ph, "onesc", [128, 2], BF16)
    P.op("gpsimd", lambda e: e.memset(onesc[:, :], 1.0), writes=["onesc"])
    epsD = sb(ph, "epsD", [128, 1])
    P.op("gpsimd", lambda e: e.memset(epsD[:, :], EPS), writes=["epsD"])
    p_hm = ps(ph, "p_hm", [128, 1024], BF16)

    P.dma("sync", lambda e: e.dma_start(out=maskT[:], in_=T["maskT_d"]), writes=["maskT"])
    P.op("gpsimd", lambda e: e.memset(C32[:, :, :, :].rearrange("p h d v -> p (h d v)"), 0.0),
         writes=[("C32", h) for h in range(4)])
    P.op("gpsimd", lambda e: e.memset(Cbf[:, :, :, :].rearrange("p h d v -> p (h d v)"), 0.0),
         writes=[("Cbf", h) for h in range(4)])
    P.op("gpsimd", lambda e: e.memset(n32[:, :, :].rearrange("p h d -> p (h d)"), 0.0),
         writes=[("n32", h) for h in range(4)])
    P.op("gpsimd", lambda e: e.memset(nbf[:, :, :].rearrange("p h d -> p (h d)"), 0.0),
         writes=[("nbf", h) for h in range(4)])

    nchunks = k.cfg.get("d1_chunks", NT)

    def load_super(J):
        b = J % 2
        P.dma("sync", lambda e: e.dma_start(
            out=qk[b][:, :, :], in_=T["qkm_T"][:, J * 512:(J + 1) * 512].rearrange("(c p) t -> p c t", p=128)),
            writes=[("qk", b)])

    def load_chunk(c):
        b = c % 2
        r = slice(c * 128, (c + 1) * 128)
        P.dma("sync", lambda e: e.dma_start(out=vt[b][:, :], in_=T["vm_s"][r, :]), writes=[("vt", b)])
        P.dma("sync", lambda e: e.dma_start(out=gt[b][:, :], in_=T["gom_s"][r, :]), writes=[("gt", b)])
        P.dma("sync", lambda e: e.dma_start(out=zt[b][:, :], in_=T["zm_s"][r, :]), writes=[("zt", b)])

    pend = []

    def hm_transposes(c, h, w):
        J, tl = c // 4, c % 4
        sbi = J % 2
        tsl = slice(tl * 128, (tl + 1) * 128)
        for vc in range(4):
            P.op("tensor", lambda e, vc=vc: e.transpose(
                p_hm[:, vc * 128:(vc + 1) * 128], hm[w][:, vc * 128:(vc + 1) * 128], ident_bf[:]),
                reads=[("hm", w)], writes=["p_hm"])
        P.op("scalar", lambda e: e.activation(
            out=hmT[sbi][:, h * 4:(h + 1) * 4, tsl],
            in_=p_hm[:, 0:512].rearrange("p (v t) -> p v t", t=128), func=AF.Copy),
            reads=["p_hm"], writes=[("hmT", sbi)])

    def do_step(part, c, h, w, J, tl, qb, cb, tsl):
        hs = slice(h * 512, (h + 1) * 512)

        def qT(d):
            return qk[qb][:, 2 * h + d, tsl]

        def kT(d):
            return qk[qb][:, 8 + 2 * h + d, tsl]

        if part == "A":
            do_step_A(c, h, w, qb, qT, kT)
        else:
            do_step_B(c, h, w, cb, qb, hs, qT)

    def do_step_A(c, h, w, qb, qT, kT):
        for d in range(2):
            P.op("tensor", lambda e, d=d: e.matmul(p_sT[:, 0:128], lhsT=kT(d), rhs=qT(d),
                                                  start=(d == 0), stop=(d == 1)),
                 reads=[("qk", qb)], writes=["p_sT"])
        for d in range(2):
            P.op("tensor", lambda e, d=d: e.transpose(p_kt[:, d * 128:(d + 1) * 128], kT(d), ident_bf[:]),
                 reads=[("qk", qb)], writes=["p_kt"])
        P.op("vector", lambda e, w=w: e.scalar_tensor_tensor(
            out=sTm[w][:, :], in0=p_sT[:, 0:128], scalar=wkcol[:, c, h:h + 1], in1=maskT[:, :],
            op0=ALU.mult, op1=ALU.mult), reads=["p_sT", "maskT"], writes=[("sTm", w)])
        P.op("scalar", lambda e, w=w: e.activation(out=ktm[w][:, :], in_=p_kt[:, 0:256], func=AF.Identity,
                                                   scale=wkcol[:, c, h:h + 1]), reads=["p_kt"], writes=[("ktm", w)])

    def do_step_B(c, h, w, cb, qb, hs, qT):
        P.op("tensor", lambda e, h=h: e.matmul(p_num[:, :], lhsT=qT(0), rhs=Cbf[:, h, 0, :], start=True, stop=False),
             reads=[("qk", qb), ("Cbf", h)], writes=["p_num"])
        P.op("tensor", lambda e, h=h: e.matmul(p_num[:, :], lhsT=qT(1), rhs=Cbf[:, h, 1, :], start=False, stop=False),
             reads=[("qk", qb), ("Cbf", h)], writes=["p_num"])
        P.op("tensor", lambda e, w=w: e.matmul(p_num[:, :], lhsT=sTm[w][:, :], rhs=vt[cb][:, hs], start=False, stop=True),
             reads=[("sTm", w), ("vt", cb)], writes=["p_num"])
        P.op("tensor", lambda e, h=h: e.matmul(p_den[:, 0:1], lhsT=qT(0), rhs=nbf[:, h, 0:1], start=True, stop=False),
             reads=[("qk", qb), ("nbf", h)], writes=["p_den"])
        P.op("tensor", lambda e, h=h: e.matmul(p_den[:, 0:1], lhsT=qT(1), rhs=nbf[:, h, 1:2], start=False, stop=False),
             reads=[("qk", qb), ("nbf", h)], writes=["p_den"])
        P.op("tensor", lambda e, w=w: e.matmul(p_den[:, 0:1], lhsT=sTm[w][:, :], rhs=onesc[:, 0:1],
                                               start=False, stop=True),
             reads=[("sTm", w), "onesc"], writes=["p_den"])
        if c < NT - 1:
            for d in range(2):
                P.op("tensor", lambda e, w=w, d=d: e.matmul(
                    p_C[:, d * 512:(d + 1) * 512], lhsT=ktm[w][:, d * 128:(d + 1) * 128], rhs=vt[cb][:, hs],
                    start=True, stop=True),
                    reads=[("ktm", w), ("vt", cb)], writes=["p_C"])
            for d in range(2):
                P.op("tensor", lambda e, w=w, d=d: e.matmul(
                    p_pn[:, d:d + 1], lhsT=ktm[w][:, d * 128:(d + 1) * 128], rhs=onesc[:, 0:1],
                    start=True, stop=True), reads=[("ktm", w), "onesc"], writes=["p_pn"])
        while pend:
            hm_transposes(*pend.pop(0))
        P.op("vector", lambda e, w=w, c=c, h=h: e.tensor_scalar(
            out=rden[w][:, :], in0=p_den[:, 0:1], scalar1=-1.0, scalar2=necol[:, c, h:h + 1],
            op0=ALU.mult, op1=ALU.max), reads=["p_den"], writes=[("rden", w)])
        P.op("vector", lambda e, w=w: e.tensor_tensor(
            out=rden[w][:, :], in0=p_den[:, 0:1], in1=rden[w][:, :], op=ALU.max),
            reads=["p_den", ("rden", w)], writes=[("rden", w)])
        P.op("vector", lambda e, w=w: e.reciprocal(out=rden[w][:, :], in_=rden[w][:, :]),
             reads=[("rden", w)], writes=[("rden", w)])
        P.op("vector", lambda e, w=w, hs=hs: e.scalar_tensor_tensor(
            out=hg[w][:, :], in0=p_num[:, :], scalar=rden[w][:, 0:1], in1=gt[cb][:, hs],
            op0=ALU.mult, op1=ALU.mult), reads=["p_num", ("rden", w), ("gt", cb)], writes=[("hg", w)])
        P.op("scalar", lambda e, w=w: e.activation(out=junk[:, :], in_=hg[w][:, :], func=AF.Square),
             reads=[("hg", w)], writes=["junkD"])
        P.op("vector", lambda e, w=w: e.reduce_sum(out=ssq[w][:, :], in_=junk[:, :], axis=AX.X),
             reads=["junkD"], writes=[("ssq", w)])
        P.op("scalar", lambda e, w=w: e.activation(out=rsd[w][:, :], in_=ssq[w][:, :], func=AF.Sqrt,
                                                   scale=1.0 / 512, bias=epsD[:, 0:1]),
             reads=[("ssq", w), "epsD"], writes=[("rsd", w)])
        P.op("vector", lambda e, w=w: e.reciprocal(out=rsd[w][:, :], in_=rsd[w][:, :]),
             reads=[("rsd", w)], writes=[("rsd", w)])
        P.op("vector", lambda e, w=w: e.scalar_tensor_tensor(
            out=hm[w][:, :], in0=hg[w][:, :], scalar=rsd[w][:, 0:1], in1=zt[cb][:, hs],
            op0=ALU.mult, op1=ALU.mult), reads=[("hg", w), ("rsd", w), ("zt", cb)], writes=[("hm", w)])
        pend.append((c, h, w))
        if c < NT - 1:
            P.op("vector", lambda e: e.scalar_tensor_tensor(
                out=C32[:, h, :, :], in0=C32[:, h, :, :], scalar=dbc[:, h, c:c + 1],
                in1=p_C[:, :].rearrange("p (d v) -> p d v", d=2), op0=ALU.mult, op1=ALU.add),
                reads=["p_C", ("C32", h)], writes=[("C32", h)])
            P.op("scalar", lambda e: e.activation(
                out=Cbf[:, h, :, :], in_=C32[:, h, :, :], func=AF.Identity, scale=dbc[:, h, c + 1:c + 2]),
                reads=[("C32", h)], writes=[("Cbf", h)])
            P.op("vector", lambda e: e.scalar_tensor_tensor(
                out=n32[:, h, :], in0=n32[:, h, :], scalar=dbc[:, h, c:c + 1], in1=p_pn[:, 0:2],
                op0=ALU.mult, op1=ALU.add), reads=["p_pn", ("n32", h)], writes=[("n32", h)])
            P.op("scalar", lambda e: e.activation(
                out=nbf[:, h, :], in_=n32[:, h, :], func=AF.Identity, scale=dbc[:, h, c + 1:c + 2]),
                reads=[("n32", h)], writes=[("nbf", h)])

    load_super(0)
    load_chunk(0)
    def step_args(st):
        c, h = st // 4, st % 4
        J, tl = c // 4, c % 4
        return (c, h, st % 2, J, tl, J % 2, c % 2, slice(tl * 128, (tl + 1) * 128))

    nsteps = nchunks * 4
    do_step("A", *step_args(0))
    for c in range(nchunks):
        J, tl = c // 4, c % 4
        if tl == 0 and (c + 4) < nchunks:
            load_super(J + 1)
        if c + 1 < nchunks:
            load_chunk(c + 1)
        for h in range(4):
            st = c * 4 + h
            if st + 1 < nsteps:
                do_step("A", *step_args(st + 1))
            do_step("B", *step_args(st))
        if tl == 3 or c == nchunks - 1:
            while pend:
                hm_transposes(*pend.pop(0))
            sbi = J % 2
            P.dma("sync", lambda e, J=J, sbi=sbi: e.dma_start(
                out=T["hm_T"][:, J * 512:(J + 1) * 512].rearrange("(c p) t -> p c t", p=128), in_=hmT[sbi][:, :, :]),
                reads=[("hmT", sbi)])


def phase_D2(k, ph, l, last, x_src, x_dst, gateb, T, sb, ps):
    P = k.P
    wpm = sb(ph, "wpm", [128, 16, 1024], BF16)
    wpa = sb(ph, "wpa", [128, 8, 1024], BF16)
    wo = sb(ph, "wo", [128, 8, 1024], BF16)
    wst = [sb(ph, "wstD%d" % i, [128, 8, 512]) for i in range(2)]
    HMT = sb(ph, "HMT", [128, 16, 512], BF16)
    HAT = sb(ph, "HAT", [128, 8, 512], BF16)
    SGM = sb(ph, "SGM", [128, 8, 512], BF16)
    SGA = sb(ph, "SGA", [128, 8, 512], BF16)
    YT = sb(ph, "YT", [128, 8, 512], BF16)
    t1 = [sb(ph, "t1_%d" % i, [128, 512]) for i in range(2)]
    t2 = [sb(ph, "t2_%d" % i, [128, 512]) for i in range(2)]
    xt = [sb(ph, "xtD%d" % i, [128, D]) for i in range(2)]
    xn = [sb(ph, "xnD%d" % i, [128, D]) for i in range(2)]
    ot = [sb(ph, "otD%d" % i, [128, D]) for i in range(2)]
    fg = sb(ph, "fg", [128, D])
    junk = sb(ph, "junkE", [128, D])
    ss = [sb(ph, "ssE%d" % i, [128, 1]) for i in range(2)]
    rs = [sb(ph, "rsE%d" % i, [128, 1]) for i in range(2)]
    p_ym = [ps(ph, "p_ym%d" % i, [128, 512]) for i in range(2)]
    p_ya = [ps(ph, "p_ya%d" % i, [128, 512]) for i in range(2)]
    p_wo = [ps(ph, "p_wo%d" % i, [128, 512]) for i in range(4)]

    if last:
        P.dma("sync", lambda e: e.dma_start(out=fg[:], in_=T["fg_b"]), writes=["fg"])
    rounds = []
    for rh in range(2):
        for ch in range(2):
            rounds.append((T["w_pm"][l, rh * 1024:(rh + 1) * 1024, ch * 512:(ch + 1) * 512],
                           wpm[:, rh * 8:(rh + 1) * 8, ch * 512:(ch + 1) * 512], ("wpm", rh, ch)))
    for ch in range(2):
        rounds.append((T["w_pa"][l, :, ch * 512:(ch + 1) * 512], wpa[:, :, ch * 512:(ch + 1) * 512], ("wpa", ch)))
    for ch in range(2):
        rounds.append((T["w_o"][l, :, ch * 512:(ch + 1) * 512], wo[:, :, ch * 512:(ch + 1) * 512], ("wo", ch)))
    import math
    lam_init = 0.8 - 0.6 * math.exp(-0.3 * l)
    mgf = sb(ph, "mgf", [128, 16])
    agf = sb(ph, "agf", [128, 8])
    P.dma("sync", lambda e: e.dma_start(out=mgf[:], in_=T["mg_fm"][l]), writes=["mgf"])
    P.dma("sync", lambda e: e.dma_start(out=agf[:], in_=T["ag_fm"][l]), writes=["agf"])
    P.op("vector", lambda e: e.tensor_scalar(out=agf[:, :], in0=agf[:, :], scalar1=1.0 - lam_init, scalar2=None,
                                             op0=ALU.mult), reads=["agf"], writes=["agf"])
    for ri, (src, dst, key) in enumerate(rounds):
        wb = ri % 2
        P.dma("sync", lambda e, src=src, wb=wb: e.dma_start(
            out=wst[wb][:, :, :], in_=src.rearrange("(c p) n -> p c n", p=128)), writes=[("wstD", wb)])
        if key[0] == "wo":
            P.op("vector", lambda e, dst=dst, wb=wb: e.tensor_copy(out=dst, in_=wst[wb][:, :, :]),
                 reads=[("wstD", wb)], writes=[key])
        else:
            for fc in range(8):
                sc = mgf[:, key[1] * 8 + fc:key[1] * 8 + fc + 1] if key[0] == "wpm" else agf[:, fc:fc + 1]
                eng = "scalar" if fc % 2 == 0 else "vector"
                if eng == "scalar":
                    P.op("scalar", lambda e, dst=dst, wb=wb, fc=fc, sc=sc: e.activation(
                        out=dst[:, fc, :], in_=wst[wb][:, fc, :], func=AF.Identity, scale=sc),
                        reads=[("wstD", wb), "mgf", "agf"], writes=[key + (fc,)])
                else:
                    P.op("vector", lambda e, dst=dst, wb=wb, fc=fc, sc=sc: e.tensor_scalar(
                        out=dst[:, fc, :], in0=wst[wb][:, fc, :], scalar1=sc, scalar2=None, op0=ALU.mult),
                        reads=[("wstD", wb), "mgf", "agf"], writes=[key + (fc,)])

    nJ = k.cfg.get("d2_tiles", 8)

    def load_J(J):
        cs = slice(J * 512, (J + 1) * 512)
        P.dma("sync", lambda e: e.dma_start(
            out=HMT[:, :, :], in_=T["hm_T"][:, cs].rearrange("(c p) t -> p c t", p=128)), writes=["HMT"])
        P.dma("sync", lambda e: e.dma_start(
            out=HAT[:, :, :], in_=T["ha_T"][:, cs].rearrange("(c p) t -> p c t", p=128)), writes=["HAT"])
        P.dma("sync", lambda e: e.dma_start(
            out=SGM[:, :, :], in_=T["sgm_T"][:, cs].rearrange("(c p) t -> p c t", p=128)), writes=["SGM"])
        P.dma("sync", lambda e: e.dma_start(
            out=SGA[:, :, :], in_=T["sga_T"][:, cs].rearrange("(c p) t -> p c t", p=128)), writes=["SGA"])

    def load_x(t):
        b = t % 2
        r = slice(t * 128, (t + 1) * 128)
        P.dma("sync", lambda e: e.dma_start(out=xt[b][:, :], in_=x_src[r, :]), writes=[("xtD", b)])

    load_J(0)
    for J in range(nJ):
        for oc in range(8):
            w = oc % 2
            osl = slice(oc * 128, (oc + 1) * 128)
            for fc in range(16):
                P.op("tensor", lambda e, fc=fc, osl=osl, w=w: e.matmul(
                    p_ym[w][:, :], lhsT=wpm[:, fc, osl], rhs=HMT[:, fc, :], start=(fc == 0), stop=(fc == 15)),
                    reads=[("wpm", fc // 8, oc // 4, fc % 8), "HMT"], writes=[("p_ym", w)])
            for fc in range(8):
                P.op("tensor", lambda e, fc=fc, osl=osl, w=w: e.matmul(
                    p_ya[w][:, :], lhsT=wpa[:, fc, osl], rhs=HAT[:, fc, :], start=(fc == 0), stop=(fc == 7)),
                    reads=[("wpa", oc // 4, fc), "HAT"], writes=[("p_ya", w)])
            P.op("vector", lambda e, w=w, oc=oc: e.tensor_tensor(out=t1[w][:, :], in0=p_ym[w][:, :], in1=SGM[:, oc, :],
                                                                 op=ALU.mult),
                 reads=[("p_ym", w), "SGM"], writes=[("t1", w)])
            P.op("vector", lambda e, w=w, oc=oc: e.tensor_tensor(out=t2[w][:, :], in0=p_ya[w][:, :], in1=SGA[:, oc, :],
                                                                 op=ALU.mult),
                 reads=[("p_ya", w), "SGA"], writes=[("t2", w)])
            P.op("vector", lambda e, w=w, oc=oc: e.tensor_tensor(out=YT[:, oc, :], in0=t1[w][:, :], in1=t2[w][:, :],
                                                                 op=ALU.add),
                 reads=[("t1", w), ("t2", w)], writes=[("YT", oc)])
        load_x(J * 4)
        load_x(J * 4 + 1)
        if J + 1 < nJ:
            load_J(J + 1)
        for tl in range(4):
            t = J * 4 + tl
            b = t % 2
            r = slice(t * 128, (t + 1) * 128)
            if tl >= 2:
                load_x(t)
            for half in range(2):
                pw = (tl * 2 + half) % 4
                hsl = slice(half * 512, (half + 1) * 512)
                for oc in range(8):
                    P.op("tensor", lambda e, oc=oc, tl=tl, pw=pw, hsl=hsl: e.matmul(
                        p_wo[pw][:, :], lhsT=YT[:, oc, tl * 128:(tl + 1) * 128], rhs=wo[:, oc, hsl],
                        start=(oc == 0), stop=(oc == 7)),
                        reads=[("YT", oc), ("wo", half)], writes=[("p_wo", pw)])
                P.op("vector", lambda e, b=b, pw=pw, hsl=hsl: e.tensor_tensor(
                    out=xn[b][:, hsl], in0=p_wo[pw][:, :], in1=gateb[:, l, hsl], op=ALU.mult),
                    reads=[("p_wo", pw)], writes=[("xnD", b, half)])
                P.op("vector", lambda e, b=b, hsl=hsl: e.tensor_tensor(
                    out=xn[b][:, hsl], in0=xn[b][:, hsl], in1=xt[b][:, hsl], op=ALU.add),
                    reads=[("xnD", b, half), ("xtD", b)], writes=[("xnD", b, half)])
            if not last:
                P.dma("sync", lambda e, b=b, r=r: e.dma_start(out=x_dst[r, :], in_=xn[b][:, :]),
                      reads=[("xnD", b, 0), ("xnD", b, 1)])
            else:
                P.op("scalar", lambda e, b=b: e.activation(out=junk[:, :], in_=xn[b][:, :], func=AF.Square),
                     reads=[("xnD", b, 0), ("xnD", b, 1)], writes=["junkE"])
                P.op("vector", lambda e, b=b: e.reduce_sum(out=ss[b][:, :], in_=junk[:, :], axis=AX.X),
                     reads=["junkE"], writes=[("ssE", b)])
                P.op("vector", lambda e, b=b: e.tensor_scalar(out=rs[b][:, :], in0=ss[b][:, :], scalar1=1.0 / D,
                                                              scalar2=EPS, op0=ALU.mult, op1=ALU.add),
                     reads=[("ssE", b)], writes=[("rsE", b)])
                P.op("scalar", lambda e, b=b: e.activation(out=rs[b][:, :], in_=rs[b][:, :], func=AF.Sqrt),
                     reads=[("rsE", b)], writes=[("rsE", b)])
                P.op("vector", lambda e, b=b: e.reciprocal(out=rs[b][:, :], in_=rs[b][:, :]),
                     reads=[("rsE", b)], writes=[("rsE", b)])
                P.op("vector", lambda e, b=b: e.scalar_tensor_tensor(
                    out=ot[b][:, :], in0=xn[b][:, :], scalar=rs[b][:, 0:1], in1=fg[:, :],
                    op0=ALU.mult, op1=ALU.mult),
                    reads=[("xnD", b, 0), ("xnD", b, 1), ("rsE", b), "fg"], writes=[("otD", b)])
                P.dma("sync", lambda e, b=b, r=r: e.dma_start(out=x_dst[r, :], in_=ot[b][:, :]),
                      reads=[("otD", b)])
```
